# Optimizing a Trainium2 kernel written in Bass

```python
import jax
import jax.numpy as jnp
from jax import lax
import numpy as np


D_MODEL = 1024
BATCH = 8
SEQ = 4096
DEPTH = 4

GRID_W = 64
CTX_LEN = 256
D_MIX = D_MODEL
D_A = D_MIX // 2
N_HEADS_A = 8
HEAD_A = D_A // N_HEADS_A
CONV_A = 4
LRU_C = 8.0
D_B = D_MIX // 4
POOL_WINDOWS = (2, 4, 8, 16)
N_GROUPS_B = len(POOL_WINDOWS)
GROUP_B = D_B // N_GROUPS_B
D_C = D_MIX // 4
N_GROUPS_C = 4
GROUP_C = D_C // N_GROUPS_C
CHUNK = 128
D_IN = 2 * D_A + D_B + 2 * D_C
D_FF = 2816
FFN_CONV = 3
N_MOD = 6
EPS = 1e-6

kernel_name = 'hybrid_pool_lru_gmlp_diffusion_block'


def rmsnorm(x, g):
    xf = x.astype(jnp.float32)
    y = xf * lax.rsqrt(jnp.mean(xf * xf, axis=-1, keepdims=True) + EPS)
    return (y * g.astype(jnp.float32)).astype(x.dtype)


def modulate(x, g, shift, scale):
    return rmsnorm(x, g) * (1 + scale) + shift


def dwconv_centred(x, w, b):
    L = x.shape[1]
    left = CONV_A // 2
    right = CONV_A - 1 - left
    xp = jnp.pad(x, ((0, 0), (left, right), (0, 0)))
    y = b
    for k in range(CONV_A):
        y = y + xp[:, k:k + L] * w[k]
    return y


def block_diag(xh, w, b):
    y = jnp.einsum('blhi,hij->blhj', xh, w)
    return y.reshape(y.shape[0], y.shape[1], -1) + b


def rglru_coeffs(xf, w_a, b_a, w_x, b_x, lam):
    B, L, _ = xf.shape
    xh = xf.reshape(B, L, N_HEADS_A, HEAD_A)
    r = jax.nn.sigmoid(block_diag(xh, w_a.astype(jnp.float32), b_a.astype(jnp.float32)))
    i = jax.nn.sigmoid(block_diag(xh, w_x.astype(jnp.float32), b_x.astype(jnp.float32)))
    log_a = -LRU_C * r * jax.nn.softplus(-lam.astype(jnp.float32))
    a = jnp.exp(log_a)
    bterm = jnp.sqrt(-jnp.expm1(2.0 * log_a)) * (i * xf)
    return a, bterm


def _combine(left, right):
    a_l, b_l = left
    a_r, b_r = right
    return a_l * a_r, a_r * b_l + b_r


def linear_scan(a, b, h0, reverse):
    idx = -1 if reverse else 0
    b = b.at[:, idx].add(a[:, idx] * h0)
    _, h = lax.associative_scan(_combine, (a, b), reverse=reverse, axis=1)
    return h


def rglru_bidir(xf, w_a, b_a, w_x, b_x, lam, h0_f, h0_b):
    a_f, b_f = rglru_coeffs(xf, w_a[0], b_a[0], w_x[0], b_x[0], lam[0])
    h_f = linear_scan(a_f, b_f, h0_f, False)
    a_b, b_b = rglru_coeffs(xf, w_a[1], b_a[1], w_x[1], b_x[1], lam[1])
    h_b = linear_scan(a_b, b_b, h0_b, True)
    return h_f, h_b


def pool_group(z, w, b, scale):
    B, L, _ = z.shape
    zf = z.astype(jnp.float32)
    cs = jnp.concatenate([jnp.zeros_like(zf[:, :1]), jnp.cumsum(zf, axis=1)], axis=1)
    t = jnp.arange(L)
    parts = []
    for g, win in enumerate(POOL_WINDOWS):
        sl = slice(g * GROUP_B, (g + 1) * GROUP_B)
        lo = jnp.clip(t - win // 2, 0, L)
        hi = jnp.clip(t - win // 2 + win, 0, L)
        csg = cs[..., sl]
        mean = (csg[:, hi] - csg[:, lo]) / (hi - lo).astype(jnp.float32)[None, :, None]
        parts.append(mean - zf[..., sl])
    p = jnp.stack(parts, axis=2)
    y = jnp.einsum('blgi,gij->blgj', p, w.astype(jnp.float32)).reshape(B, L, D_B) + b
    return (y * scale).astype(z.dtype)


def gmlp_group(z, norm_g, w_s, b_s):
    B, L, _ = z.shape
    z = jax.nn.gelu(z)
    u, v = z[..., :D_C], z[..., D_C:]
    vf = v.astype(jnp.float32)
    mu = jnp.mean(vf, axis=-1, keepdims=True)
    var = jnp.mean(jnp.square(vf - mu), axis=-1, keepdims=True)
    vn = ((vf - mu) * lax.rsqrt(var + EPS) * norm_g.astype(jnp.float32)).astype(z.dtype)
    vh = vn.reshape(B, L // CHUNK, CHUNK, N_GROUPS_C, GROUP_C)
    s = jnp.einsum('gij,bnjgc->bnigc', w_s, vh) + b_s.T[None, None, :, :, None]
    return u * s.reshape(B, L, D_C)


def token_mix(z, y_rec, pool_w, pool_b, pool_scale, gmlp_norm, gmlp_w_s, gmlp_b_s, w_out):
    ya = jax.nn.gelu(z[..., D_A:2 * D_A]) * y_rec.astype(z.dtype)
    yb = pool_group(z[..., 2 * D_A:2 * D_A + D_B], pool_w, pool_b, pool_scale)
    yc = gmlp_group(z[..., 2 * D_A + D_B:], gmlp_norm, gmlp_w_s, gmlp_b_s)
    return jnp.concatenate([ya, yb, yc], axis=-1) @ w_out


def conv_ffn(h, w_up, conv_w, conv_b, w_down, rows):
    B, L, _ = h.shape
    z = (h @ w_up).reshape(B, rows, L // rows, 2 * D_FF)
    z = lax.conv_general_dilated(
        z, conv_w[:, :, None, :].astype(z.dtype), window_strides=(1, 1), padding='SAME',
        dimension_numbers=('NHWC', 'HWIO', 'NHWC'), feature_group_count=2 * D_FF) + conv_b
    z = z.reshape(B, L, 2 * D_FF)
    return (jax.nn.gelu(z[..., :D_FF]) * z[..., D_FF:]) @ w_down


def setup_inputs(seed: int = 0) -> dict:
    key = jax.random.key(seed)
    ks = jax.random.split(key, 29)
    f32 = jnp.float32

    def nrm(k, shape, s):
        return jax.random.normal(k, shape, f32) * s

    def gain(k, shape):
        return 1.0 + 0.1 * jax.random.normal(k, shape, f32)

    u = jax.random.uniform(ks[16], (DEPTH, 2, D_A), f32, 0.9, 0.999)
    sig = u ** (1.0 / LRU_C)
    return {
        'x': nrm(ks[0], (BATCH, SEQ, D_MODEL), 1.0),
        'c': nrm(ks[1], (BATCH, D_MODEL), 1.0),
        'ctx': nrm(ks[2], (BATCH, CTX_LEN, D_MODEL), 1.0),
        'c_ctx': nrm(ks[3], (D_MODEL,), 1.0),
        'w_mod': nrm(ks[4], (DEPTH, D_MODEL, N_MOD * D_MODEL), D_MODEL ** -0.5),
        'b_mod': nrm(ks[5], (DEPTH, N_MOD * D_MODEL), 0.02),
        'g_pre_mix': gain(ks[6], (DEPTH, D_MODEL)),
        'g_post_mix': gain(ks[7], (DEPTH, D_MODEL)),
        'g_pre_ffn': gain(ks[8], (DEPTH, D_MODEL)),
        'g_post_ffn': gain(ks[9], (DEPTH, D_MODEL)),
        'w_in': nrm(ks[10], (DEPTH, D_MODEL, D_IN), D_MODEL ** -0.5),
        'conv_a_w': nrm(ks[11], (DEPTH, CONV_A, D_A), CONV_A ** -0.5),
        'conv_a_b': nrm(ks[12], (DEPTH, D_A), 0.02),
        'lru_w_a': nrm(ks[13], (DEPTH, 2, N_HEADS_A, HEAD_A, HEAD_A), HEAD_A ** -0.5),
        'lru_b_a': nrm(ks[14], (DEPTH, 2, D_A), 0.02),
        'lru_w_x': nrm(ks[15], (DEPTH, 2, N_HEADS_A, HEAD_A, HEAD_A), HEAD_A ** -0.5),
        'lru_b_x': nrm(ks[17], (DEPTH, 2, D_A), 0.02),
        'lru_lam': jnp.log(sig) - jnp.log1p(-sig),
        'pool_w': nrm(ks[18], (DEPTH, N_GROUPS_B, GROUP_B, GROUP_B), GROUP_B ** -0.5),
        'pool_b': nrm(ks[19], (DEPTH, D_B), 0.02),
        'pool_scale': gain(ks[20], (DEPTH, D_B)),
        'gmlp_norm': gain(ks[21], (DEPTH, D_C)),
        'gmlp_w_s': nrm(ks[22], (DEPTH, N_GROUPS_C, CHUNK, CHUNK), CHUNK ** -0.5),
        'gmlp_b_s': gain(ks[23], (DEPTH, N_GROUPS_C, CHUNK)),
        'w_out': nrm(ks[24], (DEPTH, D_MIX, D_MODEL), D_MIX ** -0.5),
        'ffn_w_up': nrm(ks[25], (DEPTH, D_MODEL, 2 * D_FF), D_MODEL ** -0.5),
        'ffn_conv_w': nrm(ks[26], (DEPTH, FFN_CONV, FFN_CONV, 2 * D_FF), 1.0 / FFN_CONV),
        'ffn_conv_b': nrm(ks[27], (DEPTH, 2 * D_FF), 0.02),
        'ffn_w_down': nrm(ks[28], (DEPTH, D_FF, D_MODEL), D_FF ** -0.5),
    }


def reference(x, c, ctx, c_ctx, w_mod, b_mod, g_pre_mix, g_post_mix, g_pre_ffn, g_post_ffn,
              w_in, conv_a_w, conv_a_b, lru_w_a, lru_b_a, lru_w_x, lru_b_x, lru_lam,
              pool_w, pool_b, pool_scale, gmlp_norm, gmlp_w_s, gmlp_b_s, w_out,
              ffn_w_up, ffn_conv_w, ffn_conv_b, ffn_w_down):
    B, L, _ = x.shape
    rows = L // GRID_W
    silu_c = jax.nn.silu(c)
    silu_cc = jax.nn.silu(c_ctx)
    zero_state = jnp.zeros((B, D_A), jnp.float32)
    h_lat, h_ctx = x, ctx
    for l in range(DEPTH):
        last = l == DEPTH - 1
        mod_l = jnp.split((silu_c @ w_mod[l] + b_mod[l]).reshape(B, 1, N_MOD * D_MODEL), N_MOD, axis=-1)
        mod_c = jnp.split((silu_cc @ w_mod[l] + b_mod[l]).reshape(1, 1, N_MOD * D_MODEL), N_MOD, axis=-1)

        hl = modulate(h_lat, g_pre_mix[l], mod_l[0], mod_l[1])
        hc = modulate(h_ctx, g_pre_mix[l], mod_c[0], mod_c[1])
        zl = hl @ w_in[l]
        zc = hc @ (w_in[l][:, :D_A] if last else w_in[l])

        xa_c = dwconv_centred(zc[..., :D_A], conv_a_w[l], conv_a_b[l]).astype(jnp.float32)
        hf_c, hb_c = rglru_bidir(xa_c, lru_w_a[l], lru_b_a[l], lru_w_x[l], lru_b_x[l], lru_lam[l],
                                 zero_state, zero_state)
        xa_l = dwconv_centred(zl[..., :D_A], conv_a_w[l], conv_a_b[l]).astype(jnp.float32)
        hf_l, hb_l = rglru_bidir(xa_l, lru_w_a[l], lru_b_a[l], lru_w_x[l], lru_b_x[l], lru_lam[l],
                                 hf_c[:, -1], hb_c[:, 0])

        mix_l = token_mix(zl, hf_l + hb_l, pool_w[l], pool_b[l], pool_scale[l],
                          gmlp_norm[l], gmlp_w_s[l], gmlp_b_s[l], w_out[l])
        h_lat = h_lat + mod_l[2] * rmsnorm(mix_l, g_post_mix[l])
        if not last:
            mix_c = token_mix(zc, hf_c + hb_c, pool_w[l], pool_b[l], pool_scale[l],
                              gmlp_norm[l], gmlp_w_s[l], gmlp_b_s[l], w_out[l])
            h_ctx = h_ctx + mod_c[2] * rmsnorm(mix_c, g_post_mix[l])

        hl = modulate(h_lat, g_pre_ffn[l], mod_l[3], mod_l[4])
        f_l = conv_ffn(hl, ffn_w_up[l], ffn_conv_w[l], ffn_conv_b[l], ffn_w_down[l], rows)
        h_lat = h_lat + mod_l[5] * rmsnorm(f_l, g_post_ffn[l])
        if not last:
            hc = modulate(h_ctx, g_pre_ffn[l], mod_c[3], mod_c[4])
            f_c = conv_ffn(hc, ffn_w_up[l], ffn_conv_w[l], ffn_conv_b[l], ffn_w_down[l], 1)
            h_ctx = h_ctx + mod_c[5] * rmsnorm(f_c, g_post_ffn[l])
    return h_lat
```

```python
import contextlib
import numpy as np
import concourse.bass as bass
import concourse.mybir as mybir
from concourse.bass_utils import run_bass_kernel_spmd

F32 = mybir.dt.float32
BF16 = mybir.dt.bfloat16
AF = mybir.ActivationFunctionType
ALU = mybir.AluOpType

D = 1024
KC = 8
D_A = 512
D_FF = 2816
NUP = 44
NDN = 22
EPS = 1e-6
T = 512
HM = 8
HF = 64
SB_BASE = 16512
SB_TOP = 229344

VO = {}
_n = 0
for _name, _k in [("b_mod", 48), ("g_pre_mix", 8), ("g_post_mix", 8), ("g_pre_ffn", 8),
                  ("g_post_ffn", 8), ("conv_a_w", 16), ("conv_a_b", 4), ("lru_b_a", 8),
                  ("lru_b_x", 8), ("lru_lam", 8), ("pool_b", 2), ("pool_scale", 2),
                  ("gmlp_norm", 2), ("ffn_conv_w", 9 * NUP), ("ffn_conv_b", NUP)]:
    VO[_name] = _n
    _n += _k
NV = _n
CO_ID, CO_INVW, CO_EL, CO_ER, NCONST = 0, 128, 130, 146, 162


class Tk:
    __slots__ = ("name", "w", "rd")

    def __init__(self, name):
        self.name = name
        self.w = None
        self.rd = {}


class _Eng:
    def __init__(self, name, eng):
        self.name = name
        self.eng = eng
        self.sem = None
        self.cnt = 0
        self.waited = {}


class Sched:
    def __init__(self, nc, es, n_dma_sems=12):
        self.nc = nc
        self.es = es
        self.nsem = 0
        self.keep = []
        self.engs = {}
        for n, e in [("pe", nc.tensor), ("act", nc.scalar), ("dve", nc.vector),
                     ("pool", nc.gpsimd), ("sp", nc.sync)]:
            self.engs[n] = _Eng(n, e)
            self._new_sem(self.engs[n])
        self.dsem = {}
        self.drr = {}
        for q in ("sp", "pool", "act"):
            self.dsem[q] = [[self._sem(f"d_{q}{i}"), 0] for i in range(n_dma_sems)]
            self.drr[q] = 0
        self.n_ins = 0
        self.n_wait = 0

    def _sem(self, name):
        s = self.es.enter_context(self.nc.semaphore(name))
        self.keep.append(s)
        return s

    def _new_sem(self, e):
        e.sem = self._sem(f"s_{e.name}_{self.nsem}")
        self.nsem += 1
        e.cnt = 0

    def new_epoch(self):
        for e in self.engs.values():
            if e.cnt > 0:
                self._new_sem(e)

    def _wait(self, e, rec):
        sem, val, _src = rec
        k = id(sem)
        if e.waited.get(k, 0) >= val:
            return
        e.eng.wait_ge(sem, val)
        e.waited[k] = val
        self.n_wait += 1

    def _deps(self, e, en, rd, wr, is_dma):
        if en != "pe":
            is_dma = True
        for t in rd:
            if t.w is not None:
                if is_dma or t.w[2] != en or en != "pe":
                    self._wait(e, t.w)
        for t in wr:
            if t.w is not None and (is_dma or t.w[2] != en):
                self._wait(e, t.w)
            for r in t.rd.values():
                if is_dma or r[2] != en:
                    self._wait(e, r)

    @staticmethod
    def _mark(rec, rd, wr):
        for t in rd:
            t.rd[id(rec[0])] = rec
        for t in wr:
            t.w = rec
            t.rd = {}

    def op(self, en, fn, rd=(), wr=(), inc=True):
        e = self.engs[en]
        self._deps(e, en, rd, wr, False)
        ins = fn()
        self.n_ins += 1
        if inc:
            e.cnt += 1
            ins.then_inc(e.sem, 1)
            rec = (e.sem, e.cnt, en)
        else:
            rec = (e.sem, e.cnt + 1, en)
        self._mark(rec, rd, wr)
        return ins

    def dma(self, q, out, in_, rd=(), wr=()):
        e = self.engs[q]
        self._deps(e, q, rd, wr, True)
        pool = self.dsem[q]
        k = self.drr[q]
        self.drr[q] = (k + 1) % len(pool)
        sem, val = pool[k]
        if val > 0:
            self._wait(e, (sem, val, "dma"))
        ins = e.eng.dma_start(out=out, in_=in_)
        ins.then_inc(sem, 16)
        self.n_ins += 1
        pool[k][1] = val + 16
        rec = (sem, val + 16, "dma")
        self._mark(rec, rd, wr)
        return ins

    def barrier(self):
        recs = [(o.sem, o.cnt, o.name) for o in self.engs.values() if o.cnt > 0]
        for q in self.dsem:
            for sem, val in self.dsem[q]:
                if val > 0:
                    recs.append((sem, val, "dma"))
        for e in self.engs.values():
            for r in recs:
                if r[2] != e.name:
                    self._wait(e, r)


class SBAlloc:
    def __init__(self, nc):
        self.nc = nc
        self.p = SB_BASE
        self.peak = SB_BASE
        self.k = 0
        self.off = {}
        self.keep = []

    def alloc(self, name, cols, dt):
        nb = cols * (4 if dt == F32 else 2)
        off = (self.p + 31) // 32 * 32
        assert off + nb <= SB_TOP, f"SBUF overflow at {name}: need {off + nb - SB_TOP} more bytes"
        self.p = off + nb
        self.peak = max(self.peak, self.p)
        self.k += 1
        t = self.nc.alloc_sbuf_tensor_at(f"{name}_{self.k}", [128, cols], dt, offset=off)
        self.off[id(t)] = off
        self.keep.append(t)
        return t

    def alias(self, name, cols, dt, like):
        self.k += 1
        return self.nc.alloc_sbuf_tensor_at(f"{name}_{self.k}", [128, cols], dt, offset=self.off[id(like)])

    def mark(self):
        return self.p

    def release(self, m):
        self.p = m


def V(t, c0, dims, p0=0, npart=128):
    C = t.shape[1]
    return bass.AP(t, p0 * C + c0, [[C, npart]] + [list(d) for d in dims])


def build(depth, SEQ, CTXL, dbg_names=()):
    assert SEQ % T == 0 and SEQ % 64 == 0 and CTXL <= T and CTXL % 128 == 0
    nc = bass.Bass("TRN2", target_bir_lowering=False)
    es = contextlib.ExitStack()
    S = Sched(nc, es, 24)
    sb = SBAlloc(nc)
    op = S.op

    def din(name, shape, dt=F32):
        return nc.dram_tensor(name, list(shape), dt, kind="ExternalInput").ap()

    xT = din("xT", [D, SEQ])
    cT = din("cT", [D, CTXL])
    cvec = din("cvec", [128, 16])
    vecs = din("vecs", [depth, 128, NV])
    consts = din("consts", [128, NCONST])
    w_mod = din("w_mod", [depth, 1024, 6144])
    w_in_r = din("w_in_r", [depth, 14, 128, 1024])
    w_out_r = din("w_out_r", [depth, 8, 128, 1024])
    w_up_r = din("w_up_r", [depth, NUP, 128, 1024])
    w_dn_r = din("w_dn_r", [depth, 8, 128, NDN * 128])
    lru_bd = din("lru_bd", [depth, 128, 16 * 128])
    pool_bd = din("pool_bd", [depth, 128, 2 * 128])
    wsT = din("wsT", [depth, 128, 4 * 128])
    bsT = din("bsT", [depth, 128, 2 * 128])
    outT = nc.dram_tensor("outT", [D, SEQ], F32, kind="ExternalOutput").ap()
    dbg_out = None
    dbg_map = {}
    if dbg_names:
        dbg_out = nc.dram_tensor("dbg", [128, 16384], F32, kind="ExternalOutput").ap()
    dbg_pos = [0]

    def scratch(name, shape):
        return nc.dram_tensor(name, list(shape), BF16, kind="Internal").ap()

    wb_in = scratch("wb_in", [depth, 14, 128, 1024])
    wb_out = scratch("wb_out", [depth, 8, 128, 1024])
    wb_up = scratch("wb_up", [depth, NUP, 128, 1024])
    wb_dn = scratch("wb_dn", [depth, 8, 128, NDN * 128])
    TAPS = [(0, 0), (0, -1), (0, 1), (-1, 0), (1, 0), (-1, -1), (-1, 1), (1, -1), (1, 1)]
    NPE = 7
    dgs = scratch("dgs", [depth, NDN, 128, 2 * NPE * 128])
    tk_dg = {}
    GRP = 4
    tk_wb = {}

    def convert_layer(l):
        for name, src, dst, nch in [("in", w_in_r, wb_in, 14), ("out", w_out_r, wb_out, 8),
                                    ("up", w_up_r, wb_up, NUP), ("dn", w_dn_r, wb_dn, 8)]:
            g = 2 if name == "dn" else GRP
            for c0 in range(0, nch, g):
                c1 = min(nch, c0 + g)
                tk = Tk(f"wb_{name}_{l}_{c0}")
                for c in range(c0, c1):
                    tk_wb[(name, l, c)] = tk
                S.dma("pool", out=dst[l, c0:c1], in_=src[l, c0:c1], wr=[tk])

    PB = [nc.alloc_psum_tensor(f"pb{i}", [128, 512], F32) for i in range(8)]
    PBk = [Tk(f"pb{i}") for i in range(8)]

    R = sb.alloc("R", KC * SEQ, F32)
    Rc = sb.alloc("Rc", KC * CTXL, F32)
    ntl = SEQ // T
    Rk = [[Tk(f"R{kc}_{i}") for i in range(ntl)] for kc in range(KC)]
    Rck = [[Tk(f"Rc{kc}")] for kc in range(KC)]
    cst = sb.alloc("cst", NCONST, F32)
    cstk = Tk("cst")
    ident = sb.alloc("ident", 128, BF16)
    ones = sb.alloc("ones", 128, BF16)
    cbk = Tk("cb")
    modv = sb.alloc("modv", depth * 96, F32)
    modk = Tk("modv")
    _vl = sb.alloc("vl", NV, F32)
    _vlk = Tk("vl")
    vl = [_vl, _vl]
    vlk = [_vlk, _vlk]
    NLV = 32 + 64
    lv = sb.alloc("lv", NLV, F32)
    lvk = Tk("lv")
    LV_HBA, LV_HBX, LV_NSPQ, LV_NSPH, LV_G = 0, 8, 16, 24, 32
    lw = [None, None]
    lwk = [None, None]
    bst = [None, None]
    bstk = [None, None]
    stf = sb.alloc("stf", 4, F32)
    stfk = Tk("stf")
    stbc = sb.alloc("stbc", 4, F32)
    stbck = Tk("stbc")
    sinb = sb.alloc("sinb", 4 * ntl, F32)
    sinbk = [Tk(f"sinb{i}") for i in range(ntl)]
    arena0 = sb.mark()

    class Seq:
        pass

    lat = Seq()
    lat.name, lat.R, lat.Rk, lat.len, lat.s, lat.grid = "lat", R, Rk, SEQ, 0, True
    lat.tiles = [(i * T, T) for i in range(ntl)]
    ctx = Seq()
    ctx.name, ctx.R, ctx.Rk, ctx.len, ctx.s, ctx.grid = "ctx", Rc, Rck, CTXL, 1, False
    ctx.tiles = [(0, CTXL)]

    def Rap(seq, kc, c0, n):
        return V(seq.R, kc * seq.len + c0, [[1, n]])

    def dbg(name, ap, tks, cols):
        if name not in dbg_names:
            return
        off = dbg_pos[0]
        assert off + cols <= 16384
        S.dma("pool", out=dbg_out[:, off:off + cols], in_=ap, rd=tks)
        dbg_map[name] = (off, cols)
        dbg_pos[0] = off + cols

    S.dma("sp", out=cst[:, :], in_=consts[:, :], wr=[cstk])
    op("dve", lambda: nc.vector.tensor_copy(out=ident[:, :], in_=cst[:, CO_ID:CO_ID + 128]), rd=[cstk], wr=[cbk])
    op("dve", lambda: nc.vector.memset(ones[:, :], 1.0), wr=[cbk])
    convert_layer(0)

    m0 = sb.mark()
    cv = sb.alloc("cv", 16, F32)
    cvk = Tk("cv")
    sc = sb.alloc("sc", 16, F32)
    wmb = [sb.alloc(f"wmb{i}", 2048, F32) for i in range(3)]
    wmk = [Tk(f"wmb{i}") for i in range(3)]
    mrow = sb.alloc("mrow", 6144, F32)
    mrowk = Tk("mrow")
    pv = [sb.alloc(f"pv{i}", NV, F32) for i in range(2)]
    pvk = [Tk(f"pv{i}") for i in range(2)]
    dgb = [sb.alloc(f"dgb{i}", 2 * NPE * 128, BF16) for i in range(2)]
    dgbk = [Tk(f"dgb{i}") for i in range(2)]
    S.dma("sp", out=cv[:, :], in_=cvec[:, :], wr=[cvk])
    op("act", lambda: nc.scalar.activation(out=sc[:, :], in_=cv[:, :], func=AF.Silu), rd=[cvk], wr=[cvk])
    for kc in range(KC):
        for i in range(ntl):
            S.dma("act", out=Rap(lat, kc, i * T, T), in_=xT[kc * 128:(kc + 1) * 128, i * T:(i + 1) * T], wr=[Rk[kc][i]])
        S.dma("act", out=Rap(ctx, kc, 0, CTXL), in_=cT[kc * 128:(kc + 1) * 128, :], wr=[Rck[kc][0]])
    for l in range(depth):
        lp = l % 2
        S.dma("sp", out=pv[lp][:, :], in_=vecs[l], wr=[pvk[lp]])
        nld = 0
        for cg in range(3):
            for kc in range(KC):
                b = nld % 3
                nld += 1
                S.dma("sp", out=wmb[b][:, :], in_=w_mod[l, kc * 128:(kc + 1) * 128, cg * 2048:(cg + 1) * 2048], wr=[wmk[b]])
                for sub in range(4):
                    op("pe", lambda: nc.tensor.matmul(PB[sub][0:2, 0:512], lhsT=sc[:, 2 * kc:2 * kc + 2], rhs=wmb[b][:, sub * 512:(sub + 1) * 512],
                                                      start=(kc == 0), stop=(kc == KC - 1)),
                       rd=[wmk[b], cvk], wr=[PBk[sub]], inc=(sub == 3))
            for sub in range(4):
                op("act", lambda: nc.scalar.activation(out=mrow[0:2, cg * 2048 + sub * 512:cg * 2048 + (sub + 1) * 512], in_=PB[sub][0:2, 0:512], func=AF.Identity),
                   wr=[mrowk, PBk[sub]])
        for j in range(48):
            op("pe", lambda: nc.tensor.transpose(out=PB[4][:, 2 * j:2 * j + 2], in_=mrow[0:2, j * 128:(j + 1) * 128], identity=cst[0:2, CO_ID:CO_ID + 2]),
               rd=[mrowk, cstk], wr=[PBk[4]])
        for s in range(2):
            op("dve", lambda: nc.vector.tensor_tensor(out=modv[:, l * 96 + s * 48:l * 96 + s * 48 + 48],
                                                      in0=V(PB[4], s, [[2, 48]]),
                                                      in1=pv[lp][:, VO["b_mod"]:VO["b_mod"] + 48], op=ALU.add),
               rd=[pvk[lp]], wr=[PBk[4], modk])
        for j in range(NDN):
            db, dbk = dgb[j % 2], dgbk[j % 2]
            for br in range(2):
                for t, (dr, dc) in enumerate(TAPS[:NPE]):
                    col = VO["ffn_conv_w"] + ((dr + 1) * 3 + (dc + 1)) * NUP + j + br * NDN
                    q = br * NPE + t
                    op("dve", lambda: nc.vector.tensor_scalar(out=db[:, q * 128:(q + 1) * 128], in0=cst[:, CO_ID:CO_ID + 128],
                                                              scalar1=pv[lp][:, col:col + 1], scalar2=None, op0=ALU.mult),
                       rd=[cstk, pvk[lp]], wr=[dbk])
            tk_dg[(l, j)] = Tk(f"dg_{l}_{j}")
            S.dma("act", out=dgs[l, j], in_=db[:, :], rd=[dbk], wr=[tk_dg[(l, j)]])
    S.barrier()
    sb.release(m0)

    def vcol(l, name, j=0, n=1):
        return vl[l % 2][:, VO[name] + j:VO[name] + j + n]

    def layer_setup(l):
        lp = l % 2
        S.dma("sp", out=vl[lp][:, :], in_=vecs[l], wr=[vlk[lp]])
        tv = [vlk[lp]]
        op("dve", lambda: nc.vector.tensor_scalar(out=lv[:, LV_HBA:LV_HBA + 8], in0=vcol(l, "lru_b_a", 0, 8), scalar1=0.5, scalar2=None, op0=ALU.mult), rd=tv, wr=[lvk])
        op("dve", lambda: nc.vector.tensor_scalar(out=lv[:, LV_HBX:LV_HBX + 8], in0=vcol(l, "lru_b_x", 0, 8), scalar1=0.5, scalar2=None, op0=ALU.mult), rd=tv, wr=[lvk])
        op("act", lambda: nc.scalar.activation(out=lv[:, LV_NSPQ:LV_NSPQ + 8], in_=vcol(l, "lru_lam", 0, 8), func=AF.Exp, scale=-1.0), rd=tv, wr=[lvk])
        op("act", lambda: nc.scalar.activation(out=lv[:, LV_NSPQ:LV_NSPQ + 8], in_=lv[:, LV_NSPQ:LV_NSPQ + 8], func=AF.Ln, bias=1.0), rd=[lvk], wr=[lvk])
        op("dve", lambda: nc.vector.tensor_scalar(out=lv[:, LV_NSPH:LV_NSPH + 8], in0=lv[:, LV_NSPQ:LV_NSPQ + 8], scalar1=-4.0, scalar2=None, op0=ALU.mult), rd=[lvk], wr=[lvk])
        op("dve", lambda: nc.vector.tensor_scalar(out=lv[:, LV_NSPQ:LV_NSPQ + 8], in0=lv[:, LV_NSPQ:LV_NSPQ + 8], scalar1=-2.0, scalar2=None, op0=ALU.mult), rd=[lvk], wr=[lvk])
        for s in range(2):
            mb = l * 96 + s * 48
            g0 = LV_G + s * 32
            op("dve", lambda: nc.vector.scalar_tensor_tensor(out=lv[:, g0:g0 + 8], in0=modv[:, mb + 8:mb + 16], scalar=1.0, in1=vcol(l, "g_pre_mix", 0, 8), op0=ALU.add, op1=ALU.mult), rd=tv + [modk], wr=[lvk])
            op("dve", lambda: nc.vector.tensor_tensor(out=lv[:, g0 + 8:g0 + 16], in0=modv[:, mb + 16:mb + 24], in1=vcol(l, "g_post_mix", 0, 8), op=ALU.mult), rd=tv + [modk], wr=[lvk])
            op("dve", lambda: nc.vector.scalar_tensor_tensor(out=lv[:, g0 + 16:g0 + 24], in0=modv[:, mb + 32:mb + 40], scalar=1.0, in1=vcol(l, "g_pre_ffn", 0, 8), op0=ALU.add, op1=ALU.mult), rd=tv + [modk], wr=[lvk])
            op("dve", lambda: nc.vector.tensor_tensor(out=lv[:, g0 + 24:g0 + 32], in0=modv[:, mb + 40:mb + 48], in1=vcol(l, "g_post_ffn", 0, 8), op=ALU.mult), rd=tv + [modk], wr=[lvk])

    def Gc(seq, which, kc):
        c = LV_G + seq.s * 32 + which * 8 + kc
        return lv[:, c:c + 1]

    def Sc(l, seq, which, kc):
        c = l * 96 + seq.s * 48 + (0 if which == 0 else 24) + kc
        return modv[:, c:c + 1]

    class Bufs:
        pass

    def alloc_norm(B, rstd, ntmp, sq=None):
        if sq is None:
            B.sq = [sb.alloc(f"sq{i}", T, BF16) for i in range(2)]
            B.sqk = [Tk(f"sq{i}") for i in range(2)]
        else:
            B.sq = [sq[0][0], sq[1][0]]
            B.sqk = [sq[0][1], sq[1][1]]
        B.rstd, B.rstdk = rstd
        B.ntmp = [ntmp[0][0], ntmp[1][0]]
        B.ntmpk = [ntmp[0][1], ntmp[1][1]]
        B.nti = 0

    def norm_rstd(B, src_fn, n, src_tks, src_psum=False):
        for kc in range(KC):
            sq, sqk = B.sq[kc % 2], B.sqk[kc % 2]
            tks = src_tks(kc)
            op("act", lambda: nc.scalar.activation(out=sq[:, 0:n], in_=src_fn(kc), func=AF.Square),
               rd=([] if src_psum else tks), wr=[sqk] + (tks if src_psum else []))
            op("pe", lambda: nc.tensor.matmul(PB[5][:, 0:n], lhsT=ones[:, :], rhs=sq[:, 0:n], start=(kc == 0), stop=(kc == KC - 1)),
               rd=[sqk, cbk], wr=[PBk[5]])
        op("act", lambda: nc.scalar.activation(out=B.rstd[:, 0:n], in_=PB[5][:, 0:n], func=AF.Ln, scale=1.0 / D, bias=EPS),
           wr=[B.rstdk, PBk[5]])
        op("act", lambda: nc.scalar.activation(out=B.rstd[:, 0:n], in_=B.rstd[:, 0:n], func=AF.Exp, scale=-0.5),
           rd=[B.rstdk], wr=[B.rstdk])

    def make_h(B, l, seq, which, c0, n, dst_fn, dst_tk):
        ti = c0 // T
        norm_rstd(B, lambda kc: Rap(seq, kc, c0, n), n, lambda kc: [seq.Rk[kc][ti]])
        for kc in range(KC):
            b = B.nti
            B.nti ^= 1
            tmp, tmpk = B.ntmp[b], B.ntmpk[b]
            op("dve", lambda: nc.vector.tensor_tensor(out=tmp[:, 0:n], in0=Rap(seq, kc, c0, n), in1=B.rstd[:, 0:n], op=ALU.mult),
               rd=[seq.Rk[kc][ti], B.rstdk], wr=[tmpk])
            if n >= 256:
                op("pool", lambda: nc.gpsimd.tensor_scalar(out=dst_fn(kc), in0=tmp[:, 0:n], scalar1=Gc(seq, 0 if which == 0 else 2, kc),
                                                           scalar2=Sc(l, seq, which, kc), op0=ALU.mult, op1=ALU.add),
                   rd=[tmpk, lvk, modk], wr=[dst_tk])
            else:
                op("act", lambda: nc.scalar.activation(out=dst_fn(kc), in_=tmp[:, 0:n], func=AF.Identity,
                                                       scale=Gc(seq, 0 if which == 0 else 2, kc), bias=Sc(l, seq, which, kc)),
                   rd=[tmpk, lvk, modk], wr=[dst_tk])

    def stage_h(B, l, seq, which, ti, H, mode):
        c0, n = seq.tiles[ti]
        W = H + T + H
        hb, hbk = B.hb, B.hbk
        make_h(B, l, seq, which, c0, n, lambda kc: hb[:, kc * W + H:kc * W + H + n], hbk)
        if c0 == 0:
            op("pool", lambda: nc.gpsimd.memset(V(hb, 0, [[W, KC], [1, H]]), 0.0), wr=[hbk])
        elif mode == "F":
            op("pool", lambda: nc.gpsimd.tensor_copy(out=V(hb, 0, [[W, KC], [1, H]]), in_=V(B.htail, 0, [[H, KC], [1, H]])),
               rd=[B.htailk], wr=[hbk])
        else:
            make_h(B, l, seq, which, c0 - H, H, lambda kc: hb[:, kc * W:kc * W + H], hbk)
        if c0 + n == seq.len:
            op("pool", lambda: nc.gpsimd.memset(V(hb, H + n, [[W, KC], [1, H]]), 0.0), wr=[hbk])
        else:
            make_h(B, l, seq, which, c0 + n, H, lambda kc: hb[:, kc * W + H + n:kc * W + H + n + H], hbk)
        if mode == "F" and c0 + n < seq.len:
            op("pool", lambda: nc.gpsimd.tensor_copy(out=V(B.htail, 0, [[H, KC], [1, H]]), in_=V(hb, n, [[W, KC], [1, H]])),
               rd=[hbk], wr=[B.htailk])

    class WStream:
        def __init__(self, name, nbuf, cols):
            self.b = [(sb.alloc(f"{name}{i}", cols, BF16), Tk(f"{name}{i}")) for i in range(nbuf)]
            self.i = 0

        def load(self, src, src_tk):
            t, tk = self.b[self.i]
            self.i = (self.i + 1) % len(self.b)
            S.dma("sp", out=t[:, :], in_=src, rd=[src_tk], wr=[tk])
            return t, tk

        def load2(self, src, src_tk, cols):
            t, tk = self.b[self.i]
            self.i = (self.i + 1) % len(self.b)
            S.dma("sp", out=t[:, 0:cols], in_=src, rd=[src_tk], wr=[tk])
            return t, tk

    def mm_chunk(wt, wtk, hb, hbk, W, H, n, bank, halo_bank=None, halo_col=0):
        for kc in range(KC):
            op("pe", lambda: nc.tensor.matmul(PB[bank][:, 0:n], lhsT=wt[:, kc * 128:(kc + 1) * 128],
                                              rhs=hb[:, kc * W + H:kc * W + H + n], start=(kc == 0), stop=(kc == KC - 1)),
               rd=[wtk, hbk], wr=[PBk[bank]], inc=(kc == KC - 1))
        if halo_bank is not None:
            for kc in range(KC):
                op("pe", lambda: nc.tensor.matmul(V(PB[halo_bank], halo_col, [[H, 2], [1, H]]), lhsT=wt[:, kc * 128:(kc + 1) * 128],
                                                  rhs=V(hb, kc * W, [[H + n, 2], [1, H]]), start=(kc == 0), stop=(kc == KC - 1)),
                   rd=[wtk, hbk], wr=[PBk[halo_bank]], inc=(kc == KC - 1))

    def evac_halo(dst, dstk, H, n, bank, halo_bank, halo_col):
        op("act", lambda: nc.scalar.activation(out=dst[:, H:H + n], in_=PB[bank][:, 0:n], func=AF.Identity), wr=[dstk, PBk[bank]])
        op("dve", lambda: nc.vector.tensor_copy(out=V(dst, 0, [[H + n, 2], [1, H]]), in_=V(PB[halo_bank], halo_col, [[H, 2], [1, H]])),
           wr=[dstk, PBk[halo_bank]])

    def post_norm_residual(B, l, seq, ti, which, cbase, nh, lhs_fn, lhs_tk_fn, rhs_fn, rhs_tk, nk, parts):
        nparts = len(parts)
        for oc in range(KC):
            bank = oc // 2
            col = (oc % 2) * 256
            for pi, (k0, k1) in enumerate(parts):
                wt, wtk = lhs_tk_fn(oc, pi)
                for k in range(k0, k1):
                    op("pe", lambda: nc.tensor.matmul(PB[bank][:, col:col + nh], lhsT=lhs_fn(wt, oc, k - k0), rhs=rhs_fn(k),
                                                      start=(k == 0), stop=(k == nk - 1)),
                       rd=[wtk, rhs_tk], wr=[PBk[bank]], inc=(k == k1 - 1))
        norm_rstd(B, lambda oc: PB[oc // 2][:, (oc % 2) * 256:(oc % 2) * 256 + nh], nh, lambda oc: [PBk[oc // 2]], src_psum=True)
        c0, n = seq.tiles[ti]
        for oc in range(KC):
            b = B.nti
            B.nti ^= 1
            tmp, tmpk = B.ntmp[b], B.ntmpk[b]
            col = (oc % 2) * 256
            op("dve", lambda: nc.vector.tensor_tensor(out=tmp[:, 0:nh], in0=PB[oc // 2][:, col:col + nh], in1=B.rstd[:, 0:nh], op=ALU.mult),
               rd=[B.rstdk], wr=[tmpk, PBk[oc // 2]])
            ra = Rap(seq, oc, c0 + cbase, nh)
            op("dve", lambda: nc.vector.scalar_tensor_tensor(out=ra, in0=tmp[:, 0:nh], scalar=Gc(seq, 1 if which == 0 else 3, oc),
                                                             in1=ra, op0=ALU.mult, op1=ALU.add),
               rd=[tmpk, lvk, seq.Rk[oc][ti]], wr=[seq.Rk[oc][ti]])

    def mix_alloc(l):
        B = Bufs()
        W = HM + T + HM
        B.W = W
        B.g = [sb.alloc(f"gt{i}", T, F32) for i in range(3)]
        B.gk = [Tk(f"gt{i}") for i in range(3)]
        B.g2 = [sb.alloc(f"gu{i}", T, F32) for i in range(3)]
        B.g2k = [Tk(f"gu{i}") for i in range(3)]
        alloc_norm(B, (B.g[2], B.gk[2]), [(B.g[0], B.gk[0]), (B.g[1], B.gk[1])])
        B.sel, B.selk = B.g[2], B.gk[2]
        B.hb = sb.alloc("hbm", KC * W, BF16)
        B.hbk = Tk("hbm")
        B.htail = sb.alloc("htail", KC * HM, BF16)
        B.htailk = Tk("htail")
        B.ws = WStream("wsm", 3, 1024)
        B.zx = [sb.alloc(f"zx{i}", W, F32) for i in range(2)]
        B.zxk = [Tk(f"zx{i}") for i in range(2)]
        B.xa = [sb.alloc(f"xa{i}", T, F32) for i in range(2)]
        B.xak = [Tk(f"xa{i}") for i in range(2)]
        B.xab = [sb.alloc(f"xab{i}", T, BF16) for i in range(2)]
        B.xabk = [Tk(f"xab{i}") for i in range(2)]
        B.hd = [sb.alloc(f"hd{i}", T, F32) for i in range(2)]
        B.hdk = [Tk(f"hd{i}") for i in range(2)]
        B.mixin = sb.alloc("mixin", 8 * T, BF16)
        B.mixk = Tk("mixin")
        B.ps1 = sb.alloc("ps1", W, F32)
        B.ps1k = Tk("ps1")
        B.ps2, B.ps2k = B.zx[0], B.zxk[0]
        B.pbf = sb.alloc("pbf", T, BF16)
        B.pbfk = Tk("pbf")
        B.ug = sb.alias("ug", 2 * T, F32, B.xa[0])
        B.ugk = [B.xak[0], B.xak[1]]
        B.vg = sb.alias("vg", 2 * T, BF16, B.xab[0])
        B.vgk = B.xabk
        B.vn = sb.alias("vn", 4 * 2 * 256, BF16, B.hd[0])
        B.vnk = B.hdk
        B.st = sb.alloc("st", 4 * 6 + 4 * 2 + 4, F32)
        B.stk = Tk("st")
        lp = l % 2
        lw[lp] = sb.alloc("lw", 16 * 128 + 2 * 128 + 4 * 128, BF16)
        lwk[lp] = Tk("lw")
        bst[lp] = sb.alloc("bst", 256, F32)
        bstk[lp] = Tk("bst")
        S.dma("pool", out=lw[lp][:, 0:2048], in_=lru_bd[l], wr=[lwk[lp]])
        S.dma("pool", out=lw[lp][:, 2048:2304], in_=pool_bd[l], wr=[lwk[lp]])
        S.dma("pool", out=lw[lp][:, 2304:2816], in_=wsT[l], wr=[lwk[lp]])
        S.dma("sp", out=bst[lp][:, :], in_=bsT[l], wr=[bstk[lp]])
        return B

    def xa_chunk(B, l, seq, ti, c):
        c0, n = seq.tiles[ti]
        W, H = B.W, HM
        lp = l % 2
        wt, wtk = B.ws.load(wb_in[l, c], tk_wb[("in", l, c)])
        bank = c % 2
        mm_chunk(wt, wtk, B.hb, B.hbk, W, H, n, bank, 2, (c % 2) * 2 * H)
        zx, zxk = B.zx[c % 2], B.zxk[c % 2]
        evac_halo(zx, zxk, H, n, bank, 2, (c % 2) * 2 * H)
        xa, xak = B.xa[c % 2], B.xak[c % 2]
        op("pool", lambda: nc.gpsimd.tensor_scalar(out=xa[:, 0:n], in0=zx[:, H - 2:H - 2 + n], scalar1=vcol(l, "conv_a_w", 0 * 4 + c),
                                                   scalar2=vcol(l, "conv_a_b", c), op0=ALU.mult, op1=ALU.add),
           rd=[zxk, vlk[lp]], wr=[xak])
        for k in range(1, 4):
            op("dve", lambda: nc.vector.scalar_tensor_tensor(out=xa[:, 0:n], in0=zx[:, H - 2 + k:H - 2 + k + n], scalar=vcol(l, "conv_a_w", k * 4 + c),
                                                             in1=xa[:, 0:n], op0=ALU.mult, op1=ALU.add),
               rd=[zxk, vlk[lp], xak], wr=[xak])
        op("dve", lambda: nc.vector.tensor_copy(out=B.xab[c % 2][:, 0:n], in_=xa[:, 0:n]), rd=[xak], wr=[B.xabk[c % 2]])

    def stage_scan_multi(B, l, seq, ti, items):
        c0, n = seq.tiles[ti]
        lp = l % 2
        ctxs = []
        for slot, (c, d, gset, init_ap, init_tks, out_state) in enumerate(items):
            g, gk = (B.g, B.gk) if gset == 0 else (B.g2, B.g2k)
            X = Bufs()
            X.c, X.d, X.j = c, d, d * 4 + c
            X.xab, X.xabk = B.xab[c % 2][:, 0:n], B.xabk[c % 2]
            X.xa, X.xak = B.xa[c % 2][:, 0:n], B.xak[c % 2]
            X.t = [t_[:, 0:n] for t_ in g]
            X.tf = g
            X.k = gk
            X.pa, X.px = (3, 4) if slot == 0 else (5, 7)
            X.init_ap, X.init_tks, X.out_state = init_ap, init_tks, out_state
            X.hd, X.hdk = B.hd[slot], B.hdk[slot]
            ctxs.append(X)
        col = lambda X, base: lv[:, base + X.j:base + X.j + 1]
        for X in ctxs:
            wa = lw[lp][:, ((X.d * 2 + 0) * 4 + X.c) * 128:((X.d * 2 + 0) * 4 + X.c) * 128 + 128]
            wx = lw[lp][:, ((X.d * 2 + 1) * 4 + X.c) * 128:((X.d * 2 + 1) * 4 + X.c) * 128 + 128]
            op("pe", lambda: nc.tensor.matmul(PB[X.pa][:, 0:n], lhsT=wa, rhs=X.xab, start=True, stop=True), rd=[lwk[lp], X.xabk], wr=[PBk[X.pa]])
            op("pe", lambda: nc.tensor.matmul(PB[X.px][:, 0:n], lhsT=wx, rhs=X.xab, start=True, stop=True), rd=[lwk[lp], X.xabk], wr=[PBk[X.px]])
        for X in ctxs:
            op("act", lambda: nc.scalar.activation(out=X.t[0], in_=PB[X.pa][:, 0:n], func=AF.Tanh, scale=0.5, bias=col(X, LV_HBA)), rd=[lvk], wr=[X.k[0], PBk[X.pa]])
        for X in ctxs:
            op("act", lambda: nc.scalar.activation(out=X.t[1], in_=PB[X.px][:, 0:n], func=AF.Tanh, scale=0.5, bias=col(X, LV_HBX)), rd=[lvk], wr=[X.k[1], PBk[X.px]])
        for X in ctxs:
            op("act", lambda: nc.scalar.activation(out=X.t[2], in_=X.t[0], func=AF.Exp, scale=col(X, LV_NSPH), bias=col(X, LV_NSPH)), rd=[X.k[0], lvk], wr=[X.k[2]])
        for X in ctxs:
            op("act", lambda: nc.scalar.activation(out=X.t[0], in_=X.t[0], func=AF.Tanh, scale=col(X, LV_NSPQ), bias=col(X, LV_NSPQ)), rd=[X.k[0], lvk], wr=[X.k[0]])
        for X in ctxs:
            op("dve", lambda: nc.vector.scalar_tensor_tensor(out=X.t[1], in0=X.t[1], scalar=1.0, in1=X.xa, op0=ALU.add, op1=ALU.mult), rd=[X.k[1], X.xak], wr=[X.k[1]])
        for X in ctxs:
            op("act", lambda: nc.scalar.activation(out=X.t[0], in_=X.t[0], func=AF.Sqrt, scale=-0.25), rd=[X.k[0]], wr=[X.k[0]])
        for X in ctxs:
            op("dve", lambda: nc.vector.scalar_tensor_tensor(out=X.t[0], in0=X.t[2], scalar=1.0, in1=X.t[0], op0=ALU.add, op1=ALU.mult), rd=[X.k[2], X.k[0]], wr=[X.k[0]])
        for X in ctxs:
            op("dve", lambda: nc.vector.tensor_tensor(out=X.t[1], in0=X.t[1], in1=X.t[0], op=ALU.mult), rd=[X.k[1], X.k[0]], wr=[X.k[1]])
        for X in ctxs:
            hd, hdk = X.hd, X.hdk
            if X.d == 0:
                op("dve", lambda: nc.vector.tensor_tensor_scan(out=hd[:, 0:n], data0=X.t[2], data1=X.t[1], initial=X.init_ap,
                                                               op0=ALU.mult, op1=ALU.add), rd=[X.k[2], X.k[1]] + X.init_tks, wr=[hdk])
                last = hd[:, n - 1:n]
            else:
                op("dve", lambda: nc.vector.tensor_tensor_scan(out=V(hd, n - 1, [[-1, n]]), data0=V(X.tf[2], n - 1, [[-1, n]]), data1=V(X.tf[1], n - 1, [[-1, n]]),
                                                               initial=X.init_ap, op0=ALU.mult, op1=ALU.add), rd=[X.k[2], X.k[1]] + X.init_tks, wr=[hdk])
                last = hd[:, 0:1]
            if X.out_state is not None:
                oap, otk = X.out_state
                op("pool", lambda: nc.gpsimd.tensor_copy(out=oap, in_=last), rd=[hdk], wr=[otk])

    def scan_item(seq, ti, c, d, gset, save_sinb):
        last_tile = (ti == len(seq.tiles) - 1)
        if seq is ctx:
            init_ap, init_tks = 0.0, []
            out_state = (stf[:, c:c + 1], stfk) if d == 0 else (stbc[:, c:c + 1], stbck)
        elif d == 0:
            init_ap, init_tks = stf[:, c:c + 1], [stfk]
            out_state = (stf[:, c:c + 1], stfk)
        else:
            if last_tile:
                init_ap, init_tks = stbc[:, c:c + 1], [stbck]
            else:
                init_ap, init_tks = sinb[:, ti * 4 + c:ti * 4 + c + 1], [sinbk[ti]]
            out_state = (sinb[:, (ti - 1) * 4 + c:(ti - 1) * 4 + c + 1], sinbk[ti - 1]) if (ti > 0 and save_sinb) else None
        return (c, d, gset, init_ap, init_tks, out_state)

    def gate_chunk(B, l, seq, ti, c):
        c0, n = seq.tiles[ti]
        wt, wtk = B.ws.load(wb_in[l, 4 + c], tk_wb[("in", l, 4 + c)])
        bank = 6 + (c % 2)
        mm_chunk(wt, wtk, B.hb, B.hbk, B.W, HM, n, bank)
        op("dve", lambda: nc.vector.tensor_tensor(out=B.hd[0][:, 0:n], in0=B.hd[0][:, 0:n], in1=B.hd[1][:, 0:n], op=ALU.add),
           rd=[B.hdk[0], B.hdk[1]], wr=[B.hdk[0]])
        g, gk = B.zx[c % 2], B.zxk[c % 2]
        op("act", lambda: nc.scalar.activation(out=g[:, 0:n], in_=PB[bank][:, 0:n], func=AF.Gelu_apprx_tanh), wr=[gk, PBk[bank]])
        op("dve", lambda: nc.vector.tensor_tensor(out=B.mixin[:, c * T:c * T + n], in0=g[:, 0:n], in1=B.hd[0][:, 0:n], op=ALU.mult),
           rd=[gk, B.hdk[0]], wr=[B.mixk])

    def stage_rest(B, l, seq, ti, nxt=None):
        c0, n = seq.tiles[ti]
        W, H = B.W, HM
        lp = l % 2
        nsub = n // 128
        p6 = PB[6][:].bitcast(BF16)
        st = B.st
        wso = 2304
        first = (c0 == 0)
        lastt = (c0 + n == seq.len)
        Wn = H + n + H

        def G1():
            for i in range(4):
                wt, wtk = B.ws.load(wb_in[l, 10 + i], tk_wb[("in", l, 10 + i)])
                bank = i % 2
                mm_chunk(wt, wtk, B.hb, B.hbk, W, H, n, bank)
                if i < 2:
                    op("act", lambda: nc.scalar.activation(out=B.ug[:, i * T:i * T + n], in_=PB[bank][:, 0:n], func=AF.Gelu_apprx_tanh),
                       wr=[B.ugk[i], PBk[bank]])
                else:
                    op("act", lambda: nc.scalar.activation(out=B.vg[:, (i - 2) * T:(i - 2) * T + n], in_=PB[bank][:, 0:n], func=AF.Gelu_apprx_tanh),
                       wr=B.vgk + [PBk[bank]])

        def G2():
            for sub in range(nsub):
                for vc in range(2):
                    op("pe", lambda: nc.tensor.transpose(out=p6[:, sub * 256 + vc * 128:sub * 256 + vc * 128 + 128],
                                                         in_=B.vg[:, vc * T + sub * 128:vc * T + sub * 128 + 128], identity=ident[:, :]),
                       rd=B.vgk + [cbk], wr=[PBk[6]])
            op("pool", lambda: nc.gpsimd.memset(B.vn[:, :], 0.0), wr=B.vnk)

        def G3():
            for sub in range(nsub):
                op("dve", lambda: nc.vector.bn_stats(out=st[:, sub * 6:sub * 6 + 6], in_=p6[:, sub * 256:sub * 256 + 256]), wr=[B.stk, PBk[6]])
            for sub in range(nsub):
                op("dve", lambda: nc.vector.bn_aggr(out=st[:, 24 + sub * 2:24 + sub * 2 + 2], in_=st[:, sub * 6:sub * 6 + 6]), rd=[B.stk], wr=[B.stk])
            op("act", lambda: nc.scalar.activation(out=st[:, 32:32 + nsub], in_=V(st, 25, [[2, nsub]]), func=AF.Ln, bias=EPS), rd=[B.stk], wr=[B.stk])
            op("act", lambda: nc.scalar.activation(out=st[:, 32:32 + nsub], in_=st[:, 32:32 + nsub], func=AF.Exp, scale=-0.5), rd=[B.stk], wr=[B.stk])

        def G4():
            for sub in range(nsub):
                for gc in range(2):
                    op("dve", lambda: nc.vector.tensor_scalar(out=V(B.vn, (sub * 2 + gc) * 256, [[192, 2], [1, 64]]),
                                                              in0=p6[:, sub * 256 + gc * 128:sub * 256 + gc * 128 + 128].rearrange("p (a b) -> p a b", b=64),
                                                              scalar1=st[:, 24 + sub * 2:24 + sub * 2 + 1], scalar2=st[:, 32 + sub:32 + sub + 1],
                                                              op0=ALU.subtract, op1=ALU.mult),
                       rd=[B.stk], wr=B.vnk + [PBk[6]])

        def G5(gc):
            bank = 7 if gc == 0 else 3
            for sub in range(nsub):
                for gh in range(2):
                    g = 2 * gc + gh
                    op("pe", lambda: nc.tensor.matmul(PB[bank][:, sub * 128:sub * 128 + 128],
                                                      lhsT=B.vn[:, (sub * 2 + gc) * 256 + gh * 128:(sub * 2 + gc) * 256 + gh * 128 + 128],
                                                      rhs=lw[lp][:, wso + g * 128:wso + g * 128 + 128], start=(gh == 0), stop=(gh == 1)),
                       rd=B.vnk + [lwk[lp]], wr=[PBk[bank]], inc=(gh == 1))
            g_, gk_ = B.g[gc], B.gk[gc]
            op("dve", lambda: nc.vector.scalar_tensor_tensor(out=g_[:, 0:n].rearrange("p (s i) -> p s i", i=128),
                                                             in0=PB[bank][:, 0:n].rearrange("p (s i) -> p s i", i=128),
                                                             scalar=vcol(l, "gmlp_norm", gc),
                                                             in1=V(bst[lp], gc * 128, [[0, nsub], [1, 128]]), op0=ALU.mult, op1=ALU.add),
               rd=[vlk[lp], bstk[lp]], wr=[gk_, PBk[bank]])
            op("dve", lambda: nc.vector.tensor_tensor(out=B.mixin[:, (6 + gc) * T:(6 + gc) * T + n], in0=g_[:, 0:n], in1=B.ug[:, gc * T:gc * T + n], op=ALU.mult),
               rd=[gk_, B.ugk[gc]], wr=[B.mixk])

        def add(eng, o, a_, b_, rd, wr):
            if eng == "pool":
                op("pool", lambda: nc.gpsimd.tensor_tensor(out=o, in0=a_, in1=b_, op=ALU.add), rd=rd, wr=wr)
            else:
                op("dve", lambda: nc.vector.tensor_tensor(out=o, in0=a_, in1=b_, op=ALU.add), rd=rd, wr=wr)

        def P1(pc):
            wt, wtk = B.ws.load(wb_in[l, 8 + pc], tk_wb[("in", l, 8 + pc)])
            bank = pc % 2
            mm_chunk(wt, wtk, B.hb, B.hbk, W, H, n, bank, 2, pc * 2 * H)
            evac_halo(B.zx[pc], B.zxk[pc], H, n, bank, 2, pc * 2 * H)

        def P2(pc):
            zp, zpk = B.zx[pc], B.zxk[pc]
            if pc == 0:
                add("dve", V(B.sel, 0, [[1, n]], 0, 64), V(zp, H - 1, [[1, n]], 0, 64), V(zp, H, [[1, n]], 0, 64), [zpk], [B.selk])
                add("dve", V(B.ps1, 1, [[1, Wn - 1]], 64, 64), V(zp, 0, [[1, Wn - 1]], 64, 64), V(zp, 1, [[1, Wn - 1]], 64, 64), [zpk], [B.ps1k])
                add("dve", V(B.sel, 0, [[1, n]], 64, 64), V(B.ps1, H - 1, [[1, n]], 64, 64), V(B.ps1, H + 1, [[1, n]], 64, 64), [B.ps1k], [B.selk])
            else:
                add("pool", V(B.ps1, 1, [[1, Wn - 1]]), V(zp, 0, [[1, Wn - 1]]), V(zp, 1, [[1, Wn - 1]]), [zpk], [B.ps1k])
                add("pool", V(B.ps2, 2, [[1, Wn - 3]]), V(B.ps1, 1, [[1, Wn - 3]]), V(B.ps1, 3, [[1, Wn - 3]]), [B.ps1k], [B.ps2k])
                add("dve", V(B.sel, 0, [[1, n]], 0, 64), V(B.ps2, H - 2, [[1, n]], 0, 64), V(B.ps2, H + 2, [[1, n]], 0, 64), [B.ps2k], [B.selk])
                add("dve", V(B.ps1, 4, [[1, Wn - 7]], 64, 64), V(B.ps2, 2, [[1, Wn - 7]], 64, 64), V(B.ps2, 6, [[1, Wn - 7]], 64, 64), [B.ps2k], [B.ps1k])
                add("dve", V(B.sel, 0, [[1, n]], 64, 64), V(B.ps1, H - 4, [[1, n]], 64, 64), V(B.ps1, H + 4, [[1, n]], 64, 64), [B.ps1k], [B.selk])
            op("dve", lambda: nc.vector.scalar_tensor_tensor(out=B.pbf[:, 0:n], in0=B.sel[:, 0:n], scalar=cst[:, CO_INVW + pc:CO_INVW + pc + 1],
                                                             in1=zp[:, H:H + n], op0=ALU.mult, op1=ALU.subtract),
               rd=[B.selk, zpk, cstk], wr=[B.pbfk])
            for edge, cc, co in ((first, 0, CO_EL), (lastt, n - 8, CO_ER)):
                if edge:
                    op("dve", lambda: nc.vector.tensor_tensor(out=B.sel[:, cc:cc + 8], in0=B.sel[:, cc:cc + 8], in1=cst[:, co + pc * 8:co + pc * 8 + 8], op=ALU.mult),
                       rd=[B.selk, cstk], wr=[B.selk])
                    op("dve", lambda: nc.vector.tensor_tensor(out=B.pbf[:, cc:cc + 8], in0=B.sel[:, cc:cc + 8], in1=zp[:, H + cc:H + cc + 8], op=ALU.subtract),
                       rd=[B.selk, zpk], wr=[B.pbfk])

        def P3(pc):
            op("pe", lambda: nc.tensor.matmul(PB[4][:, 0:n], lhsT=lw[lp][:, 2048 + pc * 128:2048 + pc * 128 + 128], rhs=B.pbf[:, 0:n], start=True, stop=True),
               rd=[lwk[lp], B.pbfk], wr=[PBk[4]])
            op("dve", lambda: nc.vector.tensor_scalar(out=B.mixin[:, (4 + pc) * T:(4 + pc) * T + n], in0=PB[4][:, 0:n],
                                                      scalar1=vcol(l, "pool_b", pc), scalar2=vcol(l, "pool_scale", pc), op0=ALU.add, op1=ALU.mult),
               rd=[vlk[lp]], wr=[B.mixk, PBk[4]])

        G1(); P1(0); G2(); P1(1); G3(); P2(0); G4(); P3(0); P2(1); G5(0); P3(1); G5(1)
        dbg(f"mixin_{seq.name}{ti}_l{l}", B.mixin[:, 0:8 * T], [B.mixk], 8 * T)
        if nxt is not None:
            stage_h(B, l, nxt[0], 0, nxt[1], HM, nxt[2])
        nh = n // 2
        for hh in range(2):
            cb_ = hh * nh
            post_norm_residual(B, l, seq, ti, 0, cb_, nh,
                               lambda wt, oc, k: wt[:, k * 128:(k + 1) * 128],
                               lambda oc, pi: B.ws.load(wb_out[l, oc], tk_wb[("out", l, oc)]),
                               lambda k: B.mixin[:, k * T + cb_:k * T + cb_ + nh], B.mixk, KC, [(0, KC)])

    def mix_tile(B, l, seq, ti, mode, dirs, full, nxt=None, need_h=True):
        if need_h:
            stage_h(B, l, seq, 0, ti, HM, mode)
        if full:
            dbg(f"h_{seq.name}{ti}_l{l}", B.hb[:, 0:KC * B.W], [B.hbk], KC * B.W)
        xa_chunk(B, l, seq, ti, 0)
        if len(dirs) == 2:
            for c in range(4):
                stage_scan_multi(B, l, seq, ti, [scan_item(seq, ti, c, 0, 0, not full), scan_item(seq, ti, c, 1, 1, not full)])
                if c < 3:
                    xa_chunk(B, l, seq, ti, c + 1)
                if full:
                    gate_chunk(B, l, seq, ti, c)
        else:
            d = dirs[0]
            xa_chunk(B, l, seq, ti, 1)
            for c in (0, 2):
                stage_scan_multi(B, l, seq, ti, [scan_item(seq, ti, c, d, 0, not full), scan_item(seq, ti, c + 1, d, 1, not full)])
                if c == 0:
                    xa_chunk(B, l, seq, ti, 2)
                    xa_chunk(B, l, seq, ti, 3)
                    if nxt is not None:
                        stage_h(B, l, nxt[0], 0, nxt[1], HM, nxt[2])
        if full:
            stage_rest(B, l, seq, ti, nxt)
        elif len(dirs) == 2 and nxt is not None:
            stage_h(B, l, nxt[0], 0, nxt[1], HM, nxt[2])

    def mix_phase(l, last):
        B = mix_alloc(l)
        first_lat = (lat, ntl - 1, "B") if ntl > 1 else (lat, 0, "F")
        mix_tile(B, l, ctx, 0, "B", (0, 1), not last, nxt=first_lat)
        for ti in range(ntl - 1, 0, -1):
            mix_tile(B, l, lat, ti, "B", (1,), False, nxt=((lat, ti - 1, "B") if ti > 1 else (lat, 0, "F")), need_h=False)
        for ti in range(ntl):
            mix_tile(B, l, lat, ti, "F", (0, 1), True, nxt=((lat, ti + 1, "F") if ti + 1 < ntl else None), need_h=False)
        return B

    def ffn_alloc():
        B = Bufs()
        W = HF + T + HF
        B.W = W
        B.gt = [[sb.alloc(f"gtf{p}{br}", T, F32) for br in range(2)] for p in range(2)]
        B.gtk = [[Tk(f"gtf{p}{br}") for br in range(2)] for p in range(2)]
        B.dg = WStream("dgs", 2, 2 * NPE * 128)
        rstd = sb.alias("rstdf", T, F32, B.dg.b[0][0])
        sqa = [(sb.alias(f"sqf{i}", T, BF16, B.gt[1][i]), B.gtk[1][i]) for i in range(2)]
        alloc_norm(B, (rstd, B.dg.b[0][1]), [(B.gt[0][0], B.gtk[0][0]), (B.gt[0][1], B.gtk[0][1])], sqa)
        B.hb = sb.alloc("hbf", KC * W, BF16)
        B.hbk = Tk("hbf")
        B.htail = sb.alloc("htailf", KC * HF, BF16)
        B.htailk = Tk("htailf")
        B.ws = WStream("wsu", 3, 1024)
        B.wd = WStream("wsd", 2, 1024)
        B.wd.b = B.wd.b + B.ws.b
        B.ZW = 1 + 10 * 65
        B.zb = [[sb.alloc(f"zb{p}{br}", B.ZW, BF16) for br in range(2)] for p in range(2)]
        B.zbk = [[Tk(f"zb{p}{br}") for br in range(2)] for p in range(2)]
        zero_zb(B)
        B.a = sb.alloc("aff", NDN * T, BF16)
        B.ak = Tk("aff")
        return B

    CONV_BANKS = [[3, 5], [6, 7]]
    DNP = [(0, 8), (8, 16), (16, 22)]

    def zero_zb(B):
        for p in range(2):
            for br in range(2):
                op("pool", lambda: nc.gpsimd.memset(B.zb[p][br][:, :], 0.0), wr=[B.zbk[p][br]])

    def ffn_tile(B, l, seq, ti, nxt=None, need_h=True):
        c0, n = seq.tiles[ti]
        W, H = B.W, HF
        lp = l % 2
        if need_h:
            stage_h(B, l, seq, 1, ti, H, "F")
        nrows = n // 64
        taps = TAPS if seq.grid else [(0, 0), (0, -1), (0, 1)]

        def up(j):
            p = j % 2
            for br in range(2):
                oc = j + br * NDN
                q = 2 * j + br
                wt, wtk = B.ws.load(wb_up[l, oc], tk_wb[("up", l, oc)])
                bank = q % 3
                hcol = (p * 2 + br) * 2 * H
                mm_chunk(wt, wtk, B.hb, B.hbk, W, H, n, bank, 4, hcol)
                zb, zbk = B.zb[p][br], B.zbk[p][br]
                if seq.grid:
                    op("act", lambda: nc.scalar.activation(out=V(zb, 1 + 65, [[65, nrows], [1, 64]]),
                                                           in_=PB[bank][:, 0:n].rearrange("p (r c) -> p r c", c=64), func=AF.Identity),
                       wr=[zbk, PBk[bank]])
                    op("dve", lambda: nc.vector.tensor_copy(out=V(zb, 1, [[(nrows + 1) * 65, 2], [1, 64]]), in_=V(PB[4], hcol, [[64, 2], [1, 64]])),
                       wr=[zbk, PBk[4]])
                else:
                    op("act", lambda: nc.scalar.activation(out=zb[:, H:H + n], in_=PB[bank][:, 0:n], func=AF.Identity), wr=[zbk, PBk[bank]])
                    op("dve", lambda: nc.vector.tensor_copy(out=V(zb, 0, [[H + n, 2], [1, H]]), in_=V(PB[4], hcol, [[H, 2], [1, H]])),
                       wr=[zbk, PBk[4]])

        def conv(j):
            p = j % 2
            dg, dgk = B.dg.load(dgs[l, j], tk_dg[(l, j)])
            taps_pe = [tp_ for tp_ in taps if tp_ in TAPS[:NPE]]
            taps_dve = [tp_ for tp_ in taps if tp_ not in TAPS[:NPE]]

            def src(zb, dr, dc):
                if seq.grid:
                    return V(zb, 1 + (1 + dr) * 65 + dc, [[65, nrows], [1, 64]])
                return zb[:, H + dc:H + dc + n]

            srcs = []
            for br in range(2):
                zb, zbk = B.zb[p][br], B.zbk[p][br]
                cb = CONV_BANKS[p][br]
                for t, (dr, dc) in enumerate(taps_pe):
                    q = br * NPE + TAPS.index((dr, dc))
                    op("pe", lambda: nc.tensor.matmul(PB[cb][:, 0:n], lhsT=dg[:, q * 128:(q + 1) * 128], rhs=src(zb, dr, dc),
                                                      start=(t == 0), stop=(t == len(taps_pe) - 1)),
                       rd=[dgk, zbk], wr=[PBk[cb]], inc=(t == len(taps_pe) - 1))
                gt, gtk = B.gt[p][br], B.gtk[p][br]
                srcs.append((gt[:, 0:n], [gtk], []) if taps_dve else (PB[cb][:, 0:n], [], [PBk[cb]]))
            for t, (dr, dc) in enumerate(taps_dve):
                for br in range(2):
                    zb, zbk = B.zb[p][br], B.zbk[p][br]
                    cb = CONV_BANKS[p][br]
                    oc = j + br * NDN
                    gt, gtk = B.gt[p][br], B.gtk[p][br]
                    g3 = gt[:, 0:n].rearrange("p (r c) -> p r c", c=64) if seq.grid else gt[:, 0:n]
                    wcol = vcol(l, "ffn_conv_w", ((dr + 1) * 3 + (dc + 1)) * NUP + oc)
                    if t == 0:
                        prev = PB[cb][:, 0:n].rearrange("p (r c) -> p r c", c=64) if seq.grid else PB[cb][:, 0:n]
                        op("dve", lambda: nc.vector.scalar_tensor_tensor(out=g3, in0=src(zb, dr, dc), scalar=wcol, in1=prev, op0=ALU.mult, op1=ALU.add),
                           rd=[zbk, vlk[lp]], wr=[gtk, PBk[cb]])
                    else:
                        op("dve", lambda: nc.vector.scalar_tensor_tensor(out=g3, in0=src(zb, dr, dc), scalar=wcol, in1=g3, op0=ALU.mult, op1=ALU.add),
                           rd=[zbk, vlk[lp], gtk], wr=[gtk])
            ocA, ocB = j, j + NDN
            gtA, gtAk = B.gt[p][0], B.gtk[p][0]
            (sA, sArd, sAwr), (sB, sBrd, sBwr) = srcs
            op("act", lambda: nc.scalar.activation(out=gtA[:, 0:n], in_=sA, func=AF.Gelu_apprx_tanh, bias=vcol(l, "ffn_conv_b", ocA)),
               rd=[vlk[lp]] + sArd, wr=[gtAk] + sAwr)
            op("dve", lambda: nc.vector.scalar_tensor_tensor(out=B.a[:, j * T:j * T + n], in0=sB, scalar=vcol(l, "ffn_conv_b", ocB),
                                                             in1=gtA[:, 0:n], op0=ALU.add, op1=ALU.mult),
               rd=[gtAk, vlk[lp]] + sBrd, wr=[B.ak] + sBwr)

        for j in range(NDN):
            up(j)
            if j > 0:
                conv(j - 1)
        conv(NDN - 1)
        if nxt is not None:
            stage_h(B, l, nxt[0], 1, nxt[1], H, "F")
        nh = n // 2
        for hh in range(2):
            cb_ = hh * nh
            post_norm_residual(B, l, seq, ti, 1, cb_, nh,
                               lambda wt, oc, k: wt[:, k * 128:(k + 1) * 128],
                               lambda oc, pi: B.wd.load2(wb_dn[l, oc, :, DNP[pi][0] * 128:DNP[pi][1] * 128], tk_wb[("dn", l, oc)], (DNP[pi][1] - DNP[pi][0]) * 128),
                               lambda k: B.a[:, k * T + cb_:k * T + cb_ + nh], B.ak, NDN, DNP)

    def ffn_phase(l, last):
        B = ffn_alloc()
        if not last:
            ffn_tile(B, l, ctx, 0)
            zero_zb(B)
        for ti in range(ntl):
            ffn_tile(B, l, lat, ti, nxt=((lat, ti + 1) if ti + 1 < ntl else None), need_h=(ti == 0))

    for l in range(depth):
        last = (l == depth - 1)
        if l > 0:
            S.new_epoch()
        layer_setup(l)
        if l + 1 < depth:
            convert_layer(l + 1)
        m = sb.mark()
        mix_phase(l, last)
        S.barrier()
        sb.release(m)
        m = sb.mark()
        ffn_phase(l, last)
        S.barrier()
        sb.release(m)

    osem = es.enter_context(nc.semaphore("osem"))
    nout = 0
    for kc in range(KC):
        for i in range(ntl):
            e = S.engs["sp"]
            S._deps(e, "sp", [Rk[kc][i]], [], True)
            nc.sync.dma_start(out=outT[kc * 128:(kc + 1) * 128, i * T:(i + 1) * T], in_=Rap(lat, kc, i * T, T)).then_inc(osem, 16)
            nout += 1
    nc.sync.wait_ge(osem, 16 * nout)
    S.barrier()
    info = dict(n_ins=S.n_ins, n_wait=S.n_wait, sb_peak=sb.peak, dbg=dbg_map)
    return nc, info


def _fm(v):
    return np.ascontiguousarray(np.asarray(v, np.float32).reshape(-1, 128).T)


def _chunk_w(w, noc):
    L = w.shape[0]
    w = np.asarray(w, np.float32).reshape(L, 8, 128, noc, 128)
    return np.ascontiguousarray(w.transpose(0, 3, 2, 1, 4).reshape(L, noc, 128, 1024))


def prepare_shared(depth, w_mod, b_mod, g_pre_mix, g_post_mix, g_pre_ffn, g_post_ffn, w_in, conv_a_w, conv_a_b,
                   lru_w_a, lru_b_a, lru_w_x, lru_b_x, lru_lam, pool_w, pool_b, pool_scale, gmlp_norm,
                   gmlp_w_s, gmlp_b_s, w_out, ffn_w_up, ffn_conv_w, ffn_conv_b, ffn_w_down):
    L = depth
    vecs = np.zeros((L, 128, NV), np.float32)
    for l in range(L):
        def put(name, arr):
            a = _fm(arr)
            vecs[l, :, VO[name]:VO[name] + a.shape[1]] = a
        put("b_mod", b_mod[l])
        put("g_pre_mix", g_pre_mix[l]); put("g_post_mix", g_post_mix[l])
        put("g_pre_ffn", g_pre_ffn[l]); put("g_post_ffn", g_post_ffn[l])
        vecs[l, :, VO["conv_a_w"]:VO["conv_a_w"] + 16] = np.concatenate([_fm(conv_a_w[l][k]) for k in range(4)], axis=1)
        put("conv_a_b", conv_a_b[l])
        for nm, arr in (("lru_b_a", lru_b_a), ("lru_b_x", lru_b_x), ("lru_lam", lru_lam)):
            vecs[l, :, VO[nm]:VO[nm] + 8] = np.concatenate([_fm(arr[l][d]) for d in range(2)], axis=1)
        put("pool_b", pool_b[l]); put("pool_scale", pool_scale[l]); put("gmlp_norm", gmlp_norm[l])
        vecs[l, :, VO["ffn_conv_w"]:VO["ffn_conv_w"] + 9 * NUP] = np.concatenate(
            [_fm(ffn_conv_w[l][kh, kw]) for kh in range(3) for kw in range(3)], axis=1)
        put("ffn_conv_b", ffn_conv_b[l])
    consts = np.zeros((128, NCONST), np.float32)
    consts[:, CO_ID:CO_ID + 128] = np.eye(128, dtype=np.float32)
    wins = [[2, 4], [8, 16]]
    for pc in range(2):
        for half in range(2):
            w = wins[pc][half]
            ps = slice(half * 64, half * 64 + 64)
            consts[ps, CO_INVW + pc] = 1.0 / w
            for t in range(8):
                consts[ps, CO_EL + pc * 8 + t] = 1.0 / min(w, t + w // 2)
                j = 7 - t
                consts[ps, CO_ER + pc * 8 + t] = 1.0 / min(w, j + 1 + w // 2)
    w_mod_c = np.ascontiguousarray(np.asarray(w_mod[:L], np.float32))
    w_in_r = _chunk_w(w_in[:L], 14)
    w_out_r = _chunk_w(w_out[:L], 8)
    w_up_r = _chunk_w(ffn_w_up[:L], NUP)
    wd = np.asarray(ffn_w_down[:L], np.float32).reshape(L, NDN, 128, 8, 128)
    w_dn_r = np.ascontiguousarray(wd.transpose(0, 3, 2, 1, 4).reshape(L, 8, 128, NDN * 128))
    lru_bd = np.zeros((L, 128, 2, 2, 4, 128), np.float32)
    for gi, wsrc in enumerate((lru_w_a, lru_w_x)):
        ws = np.asarray(wsrc[:L], np.float32)
        for c in range(4):
            for hh in range(2):
                lru_bd[:, hh * 64:hh * 64 + 64, :, gi, c, hh * 64:hh * 64 + 64] = ws[:, :, 2 * c + hh].transpose(0, 2, 1, 3)
    lru_bd = np.ascontiguousarray(lru_bd.reshape(L, 128, 16 * 128))
    pool_bdm = np.zeros((L, 128, 2, 128), np.float32)
    pw = np.asarray(pool_w[:L], np.float32)
    for pc in range(2):
        for hh in range(2):
            pool_bdm[:, hh * 64:hh * 64 + 64, pc, hh * 64:hh * 64 + 64] = pw[:, 2 * pc + hh]
    pool_bdm = np.ascontiguousarray(pool_bdm.reshape(L, 128, 256))
    wsTm = np.ascontiguousarray(np.asarray(gmlp_w_s[:L], np.float32).transpose(0, 3, 1, 2).reshape(L, 128, 512))
    bs = np.asarray(gmlp_b_s[:L], np.float32)
    bsTm = np.zeros((L, 128, 2, 128), np.float32)
    for gc in range(2):
        for hh in range(2):
            bsTm[:, hh * 64:hh * 64 + 64, gc, :] = bs[:, 2 * gc + hh][:, None, :]
    bsTm = np.ascontiguousarray(bsTm.reshape(L, 128, 256))
    return dict(vecs=vecs, consts=consts, w_mod=w_mod_c, w_in_r=w_in_r, w_out_r=w_out_r, w_up_r=w_up_r, w_dn_r=w_dn_r,
                lru_bd=lru_bd, pool_bd=pool_bdm, wsT=wsTm, bsT=bsTm)


def per_core_inputs(x, c, ctx, c_ctx, b):
    cv = np.stack([_fm(c[b]), _fm(c_ctx)], axis=2).reshape(128, 16)
    return dict(xT=np.ascontiguousarray(np.asarray(x[b], np.float32).T),
                cT=np.ascontiguousarray(np.asarray(ctx[b], np.float32).T),
                cvec=np.ascontiguousarray(cv.astype(np.float32)))


_CACHE = {}


def run(depth, x, c, ctx, c_ctx, weights, dbg_names=(), cores=None):
    Bn, SEQ, _ = x.shape
    CTXL = ctx.shape[1]
    key = (depth, SEQ, CTXL, tuple(dbg_names))
    if key not in _CACHE:
        _CACHE[key] = build(depth, SEQ, CTXL, dbg_names)
    nc, info = _CACHE[key]
    shared = prepare_shared(depth, **weights)
    cores = list(range(Bn)) if cores is None else cores
    in_maps = []
    for b in cores:
        m = dict(shared)
        m.update(per_core_inputs(x, c, ctx, c_ctx, b))
        in_maps.append(m)
    res = run_bass_kernel_spmd(nc, in_maps, core_ids=list(range(len(cores))))
    outs = [np.ascontiguousarray(r["outT"].T) for r in res.results]
    dbg = [r.get("dbg") for r in res.results] if dbg_names else None
    return np.stack(outs, axis=0), info, dbg


def kernel(x, c, ctx, c_ctx, **weights):
    x = np.asarray(x)
    out, _info, _ = run(4, x, np.asarray(c), np.asarray(ctx), np.asarray(c_ctx), {k: np.asarray(v) for k, v in weights.items()})
    return out.astype(np.float32)
```

```python
import contextlib
import numpy as np
import concourse.bass as bass
import concourse.mybir as mybir
from concourse.bass_utils import run_bass_kernel_spmd

F32 = mybir.dt.float32
BF16 = mybir.dt.bfloat16
AF = mybir.ActivationFunctionType
ALU = mybir.AluOpType

D = 1024
KC = 8
D_A = 512
D_FF = 2816
NUP = 44
NDN = 22
EPS = 1e-6
T = 512
HM = 8
HF = 64
SB_BASE = 16512
SB_TOP = 229344

VO = {}
_n = 0
for _name, _k in [("b_mod", 48), ("g_pre_mix", 8), ("g_post_mix", 8), ("g_pre_ffn", 8),
                  ("g_post_ffn", 8), ("conv_a_w", 16), ("conv_a_b", 4), ("lru_b_a", 8),
                  ("lru_b_x", 8), ("lru_lam", 8), ("pool_b", 2), ("pool_scale", 2),
                  ("gmlp_norm", 2), ("ffn_conv_w", 9 * NUP), ("ffn_conv_b", NUP)]:
    VO[_name] = _n
    _n += _k
NV = _n
CO_ID, CO_INVW, CO_EL, CO_ER, NCONST = 0, 128, 130, 146, 162


class Tk:
    __slots__ = ("name", "w", "rd")

    def __init__(self, name):
        self.name = name
        self.w = None
        self.rd = {}


class _Eng:
    def __init__(self, name, eng):
        self.name = name
        self.eng = eng
        self.sem = None
        self.cnt = 0
        self.waited = {}


class Sched:
    def __init__(self, nc, es, n_dma_sems=12):
        self.nc = nc
        self.es = es
        self.nsem = 0
        self.keep = []
        self.engs = {}
        for n, e in [("pe", nc.tensor), ("act", nc.scalar), ("dve", nc.vector),
                     ("pool", nc.gpsimd), ("sp", nc.sync)]:
            self.engs[n] = _Eng(n, e)
            self._new_sem(self.engs[n])
        self.dsem = {}
        self.drr = {}
        for q in ("sp", "pool", "act"):
            self.dsem[q] = [[self._sem(f"d_{q}{i}"), 0] for i in range(n_dma_sems)]
            self.drr[q] = 0
        self.n_ins = 0
        self.n_wait = 0
        self.hist = {}
        self.hptr = {}

    def _sem(self, name):
        s = self.es.enter_context(self.nc.semaphore(name))
        self.keep.append(s)
        return s

    def _new_sem(self, e):
        e.sem = self._sem(f"s_{e.name}_{self.nsem}")
        self.nsem += 1
        e.cnt = 0

    def new_epoch(self):
        for e in self.engs.values():
            if e.cnt > 0:
                self._new_sem(e)

    def _wait(self, e, rec):
        sem, val, _src = rec
        k = id(sem)
        if e.waited.get(k, 0) >= val:
            return
        e.eng.wait_ge(sem, val)
        e.waited[k] = val
        self.n_wait += 1
        own = self.hist.setdefault(id(e.sem), [])
        own.append((e.cnt + 1, k, val))
        stack = [(k, val)]
        while stack:
            k1, v1 = stack.pop()
            h = self.hist.get(k1)
            if not h or k1 == id(e.sem):
                continue
            p = self.hptr.get((e.name, k1), 0)
            while p < len(h) and h[p][0] <= v1:
                _c, k2, v2 = h[p]
                p += 1
                if e.waited.get(k2, 0) < v2:
                    e.waited[k2] = v2
                    own.append((e.cnt + 1, k2, v2))
                    stack.append((k2, v2))
            self.hptr[(e.name, k1)] = p

    def _deps(self, e, en, rd, wr, is_dma):
        if en != "pe":
            is_dma = True
        for t in rd:
            if t.w is not None:
                if is_dma or t.w[2] != en or en != "pe":
                    self._wait(e, t.w)
        for t in wr:
            if t.w is not None and (is_dma or t.w[2] != en):
                self._wait(e, t.w)
            for r in t.rd.values():
                if is_dma or r[2] != en:
                    self._wait(e, r)

    @staticmethod
    def _mark(rec, rd, wr):
        for t in rd:
            t.rd[id(rec[0])] = rec
        for t in wr:
            t.w = rec
            t.rd = {}

    def op(self, en, fn, rd=(), wr=(), inc=True):
        e = self.engs[en]
        self._deps(e, en, rd, wr, False)
        ins = fn()
        self.n_ins += 1
        if inc:
            e.cnt += 1
            ins.then_inc(e.sem, 1)
            rec = (e.sem, e.cnt, en)
        else:
            rec = (e.sem, e.cnt + 1, en)
        self._mark(rec, rd, wr)
        return ins

    def dma(self, q, out, in_, rd=(), wr=()):
        e = self.engs[q]
        self._deps(e, q, rd, wr, True)
        pool = self.dsem[q]
        k = self.drr[q]
        self.drr[q] = (k + 1) % len(pool)
        sem, val = pool[k]
        if val > 0:
            self._wait(e, (sem, val, "dma"))
        ins = e.eng.dma_start(out=out, in_=in_)
        ins.then_inc(sem, 16)
        self.n_ins += 1
        pool[k][1] = val + 16
        rec = (sem, val + 16, "dma")
        self._mark(rec, rd, wr)
        return ins

    def barrier(self):
        recs = [(o.sem, o.cnt, o.name) for o in self.engs.values() if o.cnt > 0]
        for q in self.dsem:
            for sem, val in self.dsem[q]:
                if val > 0:
                    recs.append((sem, val, "dma"))
        for e in self.engs.values():
            for r in recs:
                if r[2] != e.name:
                    self._wait(e, r)


class SBAlloc:
    def __init__(self, nc):
        self.nc = nc
        self.p = SB_BASE
        self.peak = SB_BASE
        self.k = 0
        self.off = {}
        self.keep = []

    def alloc(self, name, cols, dt):
        nb = cols * (4 if dt == F32 else 2)
        off = (self.p + 31) // 32 * 32
        assert off + nb <= SB_TOP, f"SBUF overflow at {name}: need {off + nb - SB_TOP} more bytes"
        self.p = off + nb
        self.peak = max(self.peak, self.p)
        self.k += 1
        t = self.nc.alloc_sbuf_tensor_at(f"{name}_{self.k}", [128, cols], dt, offset=off)
        self.off[id(t)] = off
        self.keep.append(t)
        return t

    def alias(self, name, cols, dt, like):
        self.k += 1
        return self.nc.alloc_sbuf_tensor_at(f"{name}_{self.k}", [128, cols], dt, offset=self.off[id(like)])

    def mark(self):
        return self.p

    def release(self, m):
        self.p = m


def V(t, c0, dims, p0=0, npart=128):
    C = t.shape[1]
    return bass.AP(t, p0 * C + c0, [[C, npart]] + [list(d) for d in dims])


def build(depth, SEQ, CTXL, dbg_names=()):
    assert SEQ % T == 0 and SEQ % 64 == 0 and CTXL <= T and CTXL % 128 == 0
    nc = bass.Bass("TRN2", target_bir_lowering=False)
    es = contextlib.ExitStack()
    S = Sched(nc, es, 24)
    sb = SBAlloc(nc)
    op = S.op

    def din(name, shape, dt=F32):
        return nc.dram_tensor(name, list(shape), dt, kind="ExternalInput").ap()

    xT = din("xT", [D, SEQ])
    cT = din("cT", [D, CTXL])
    cvec = din("cvec", [128, 16])
    vecs = din("vecs", [depth, 128, NV])
    consts = din("consts", [128, NCONST])
    w_mod = din("w_mod", [depth, 1024, 6144])
    w_in_r = din("w_in_r", [depth, 14, 128, 1024])
    w_out_r = din("w_out_r", [depth, 8, 128, 1024])
    w_up_r = din("w_up_r", [depth, NUP, 128, 1024])
    w_dn_r = din("w_dn_r", [depth, 8, 128, NDN * 128])
    lru_bd = din("lru_bd", [depth, 128, 16 * 128])
    pool_bd = din("pool_bd", [depth, 128, 2 * 128])
    wsT = din("wsT", [depth, 128, 4 * 128])
    bsT = din("bsT", [depth, 128, 2 * 128])
    outT = nc.dram_tensor("outT", [D, SEQ], F32, kind="ExternalOutput").ap()
    dbg_out = None
    dbg_map = {}
    if dbg_names:
        dbg_out = nc.dram_tensor("dbg", [128, 16384], F32, kind="ExternalOutput").ap()
    dbg_pos = [0]

    def scratch(name, shape):
        return nc.dram_tensor(name, list(shape), BF16, kind="Internal").ap()

    wb_in = scratch("wb_in", [depth, 14, 128, 1024])
    wb_out = scratch("wb_out", [depth, 8, 128, 1024])
    wb_up = scratch("wb_up", [depth, NUP, 128, 1024])
    wb_dn = scratch("wb_dn", [depth, 8, 128, NDN * 128])
    TAPS = [(0, 0), (0, -1), (0, 1), (-1, 0), (1, 0), (-1, -1), (-1, 1), (1, -1), (1, 1)]
    NPE = 7
    dgs = scratch("dgs", [depth, NDN, 128, 2 * NPE * 128])
    tk_dg = {}
    GRP = 4
    tk_wb = {}

    def convert_layer(l):
        for name, src, dst, nch in [("in", w_in_r, wb_in, 14), ("out", w_out_r, wb_out, 8),
                                    ("up", w_up_r, wb_up, NUP), ("dn", w_dn_r, wb_dn, 8)]:
            g = 2 if name == "dn" else GRP
            for c0 in range(0, nch, g):
                c1 = min(nch, c0 + g)
                tk = Tk(f"wb_{name}_{l}_{c0}")
                for c in range(c0, c1):
                    tk_wb[(name, l, c)] = tk
                S.dma("pool", out=dst[l, c0:c1], in_=src[l, c0:c1], wr=[tk])

    PB = [nc.alloc_psum_tensor(f"pb{i}", [128, 512], F32) for i in range(8)]
    PBk = [Tk(f"pb{i}") for i in range(8)]

    R = sb.alloc("R", KC * SEQ, F32)
    Rc = sb.alloc("Rc", KC * CTXL, F32)
    ntl = SEQ // T
    Rk = [[Tk(f"R{kc}_{i}") for i in range(ntl)] for kc in range(KC)]
    Rck = [[Tk(f"Rc{kc}")] for kc in range(KC)]
    cst = sb.alloc("cst", NCONST, F32)
    cstk = Tk("cst")
    ident = sb.alloc("ident", 128, BF16)
    ones = sb.alloc("ones", 128, BF16)
    cbk = Tk("cb")
    modv = sb.alloc("modv", depth * 96, F32)
    modk = Tk("modv")
    _vl = sb.alloc("vl", NV, F32)
    _vlk = Tk("vl")
    vl = [_vl, _vl]
    vlk = [_vlk, _vlk]
    NLV = 32 + 64
    lv = sb.alloc("lv", NLV, F32)
    lvk = Tk("lv")
    LV_HBA, LV_HBX, LV_NSPQ, LV_NSPH, LV_G = 0, 8, 16, 24, 32
    lw = [None, None]
    lwk = [None, None]
    bst = [None, None]
    bstk = [None, None]
    stf = sb.alloc("stf", 4, F32)
    stfk = Tk("stf")
    stbc = sb.alloc("stbc", 4, F32)
    stbck = Tk("stbc")
    sinb = sb.alloc("sinb", 4 * ntl, F32)
    sinbk = [Tk(f"sinb{i}") for i in range(ntl)]
    arena0 = sb.mark()

    class Seq:
        pass

    lat = Seq()
    lat.name, lat.R, lat.Rk, lat.len, lat.s, lat.grid = "lat", R, Rk, SEQ, 0, True
    lat.tiles = [(i * T, T) for i in range(ntl)]
    ctx = Seq()
    ctx.name, ctx.R, ctx.Rk, ctx.len, ctx.s, ctx.grid = "ctx", Rc, Rck, CTXL, 1, False
    ctx.tiles = [(0, CTXL)]

    def Rap(seq, kc, c0, n):
        return V(seq.R, kc * seq.len + c0, [[1, n]])

    def dbg(name, ap, tks, cols):
        if name not in dbg_names:
            return
        off = dbg_pos[0]
        assert off + cols <= 16384
        S.dma("pool", out=dbg_out[:, off:off + cols], in_=ap, rd=tks)
        dbg_map[name] = (off, cols)
        dbg_pos[0] = off + cols

    S.dma("sp", out=cst[:, :], in_=consts[:, :], wr=[cstk])
    op("dve", lambda: nc.vector.tensor_copy(out=ident[:, :], in_=cst[:, CO_ID:CO_ID + 128]), rd=[cstk], wr=[cbk])
    op("dve", lambda: nc.vector.memset(ones[:, :], 1.0), wr=[cbk])
    convert_layer(0)

    m0 = sb.mark()
    cv = sb.alloc("cv", 16, F32)
    cvk = Tk("cv")
    sc = sb.alloc("sc", 16, F32)
    wmb = [sb.alloc(f"wmb{i}", 2048, F32) for i in range(3)]
    wmk = [Tk(f"wmb{i}") for i in range(3)]
    mrow = sb.alloc("mrow", 6144, F32)
    mrowk = Tk("mrow")
    pv = [sb.alloc(f"pv{i}", NV, F32) for i in range(2)]
    pvk = [Tk(f"pv{i}") for i in range(2)]
    dgb = [sb.alloc(f"dgb{i}", 2 * NPE * 128, BF16) for i in range(2)]
    dgbk = [Tk(f"dgb{i}") for i in range(2)]
    S.dma("sp", out=cv[:, :], in_=cvec[:, :], wr=[cvk])
    op("act", lambda: nc.scalar.activation(out=sc[:, :], in_=cv[:, :], func=AF.Silu), rd=[cvk], wr=[cvk])
    for kc in range(KC):
        for i in range(ntl):
            S.dma("act", out=Rap(lat, kc, i * T, T), in_=xT[kc * 128:(kc + 1) * 128, i * T:(i + 1) * T], wr=[Rk[kc][i]])
        S.dma("act", out=Rap(ctx, kc, 0, CTXL), in_=cT[kc * 128:(kc + 1) * 128, :], wr=[Rck[kc][0]])
    for l in range(depth):
        lp = l % 2
        S.dma("sp", out=pv[lp][:, :], in_=vecs[l], wr=[pvk[lp]])
        nld = 0
        for cg in range(3):
            for kc in range(KC):
                b = nld % 3
                nld += 1
                S.dma("sp", out=wmb[b][:, :], in_=w_mod[l, kc * 128:(kc + 1) * 128, cg * 2048:(cg + 1) * 2048], wr=[wmk[b]])
                for sub in range(4):
                    op("pe", lambda: nc.tensor.matmul(PB[sub][0:2, 0:512], lhsT=sc[:, 2 * kc:2 * kc + 2], rhs=wmb[b][:, sub * 512:(sub + 1) * 512],
                                                      start=(kc == 0), stop=(kc == KC - 1)),
                       rd=[wmk[b], cvk], wr=[PBk[sub]], inc=(sub == 3))
            for sub in range(4):
                op("act", lambda: nc.scalar.activation(out=mrow[0:2, cg * 2048 + sub * 512:cg * 2048 + (sub + 1) * 512], in_=PB[sub][0:2, 0:512], func=AF.Identity),
                   wr=[mrowk, PBk[sub]])
        for j in range(48):
            op("pe", lambda: nc.tensor.transpose(out=PB[4][:, 2 * j:2 * j + 2], in_=mrow[0:2, j * 128:(j + 1) * 128], identity=cst[0:2, CO_ID:CO_ID + 2]),
               rd=[mrowk, cstk], wr=[PBk[4]])
        for s in range(2):
            op("dve", lambda: nc.vector.tensor_tensor(out=modv[:, l * 96 + s * 48:l * 96 + s * 48 + 48],
                                                      in0=V(PB[4], s, [[2, 48]]),
                                                      in1=pv[lp][:, VO["b_mod"]:VO["b_mod"] + 48], op=ALU.add),
               rd=[pvk[lp]], wr=[PBk[4], modk])
        for j in range(NDN):
            db, dbk = dgb[j % 2], dgbk[j % 2]
            for br in range(2):
                for t, (dr, dc) in enumerate(TAPS[:NPE]):
                    col = VO["ffn_conv_w"] + ((dr + 1) * 3 + (dc + 1)) * NUP + j + br * NDN
                    q = br * NPE + t
                    op("dve", lambda: nc.vector.tensor_scalar(out=db[:, q * 128:(q + 1) * 128], in0=cst[:, CO_ID:CO_ID + 128],
                                                              scalar1=pv[lp][:, col:col + 1], scalar2=None, op0=ALU.mult),
                       rd=[cstk, pvk[lp]], wr=[dbk])
            tk_dg[(l, j)] = Tk(f"dg_{l}_{j}")
            S.dma("act", out=dgs[l, j], in_=db[:, :], rd=[dbk], wr=[tk_dg[(l, j)]])
    S.barrier()
    sb.release(m0)

    def vcol(l, name, j=0, n=1):
        return vl[l % 2][:, VO[name] + j:VO[name] + j + n]

    def layer_setup(l):
        lp = l % 2
        S.dma("sp", out=vl[lp][:, :], in_=vecs[l], wr=[vlk[lp]])
        tv = [vlk[lp]]
        op("dve", lambda: nc.vector.tensor_scalar(out=lv[:, LV_HBA:LV_HBA + 8], in0=vcol(l, "lru_b_a", 0, 8), scalar1=0.5, scalar2=None, op0=ALU.mult), rd=tv, wr=[lvk])
        op("dve", lambda: nc.vector.tensor_scalar(out=lv[:, LV_HBX:LV_HBX + 8], in0=vcol(l, "lru_b_x", 0, 8), scalar1=0.5, scalar2=None, op0=ALU.mult), rd=tv, wr=[lvk])
        op("act", lambda: nc.scalar.activation(out=lv[:, LV_NSPQ:LV_NSPQ + 8], in_=vcol(l, "lru_lam", 0, 8), func=AF.Exp, scale=-1.0), rd=tv, wr=[lvk])
        op("act", lambda: nc.scalar.activation(out=lv[:, LV_NSPQ:LV_NSPQ + 8], in_=lv[:, LV_NSPQ:LV_NSPQ + 8], func=AF.Ln, bias=1.0), rd=[lvk], wr=[lvk])
        op("dve", lambda: nc.vector.tensor_scalar(out=lv[:, LV_NSPH:LV_NSPH + 8], in0=lv[:, LV_NSPQ:LV_NSPQ + 8], scalar1=-4.0, scalar2=None, op0=ALU.mult), rd=[lvk], wr=[lvk])
        op("dve", lambda: nc.vector.tensor_scalar(out=lv[:, LV_NSPQ:LV_NSPQ + 8], in0=lv[:, LV_NSPQ:LV_NSPQ + 8], scalar1=-2.0, scalar2=None, op0=ALU.mult), rd=[lvk], wr=[lvk])
        for s in range(2):
            mb = l * 96 + s * 48
            g0 = LV_G + s * 32
            op("dve", lambda: nc.vector.scalar_tensor_tensor(out=lv[:, g0:g0 + 8], in0=modv[:, mb + 8:mb + 16], scalar=1.0, in1=vcol(l, "g_pre_mix", 0, 8), op0=ALU.add, op1=ALU.mult), rd=tv + [modk], wr=[lvk])
            op("dve", lambda: nc.vector.tensor_tensor(out=lv[:, g0 + 8:g0 + 16], in0=modv[:, mb + 16:mb + 24], in1=vcol(l, "g_post_mix", 0, 8), op=ALU.mult), rd=tv + [modk], wr=[lvk])
            op("dve", lambda: nc.vector.scalar_tensor_tensor(out=lv[:, g0 + 16:g0 + 24], in0=modv[:, mb + 32:mb + 40], scalar=1.0, in1=vcol(l, "g_pre_ffn", 0, 8), op0=ALU.add, op1=ALU.mult), rd=tv + [modk], wr=[lvk])
            op("dve", lambda: nc.vector.tensor_tensor(out=lv[:, g0 + 24:g0 + 32], in0=modv[:, mb + 40:mb + 48], in1=vcol(l, "g_post_ffn", 0, 8), op=ALU.mult), rd=tv + [modk], wr=[lvk])

    def Gc(seq, which, kc):
        c = LV_G + seq.s * 32 + which * 8 + kc
        return lv[:, c:c + 1]

    def Sc(l, seq, which, kc):
        c = l * 96 + seq.s * 48 + (0 if which == 0 else 24) + kc
        return modv[:, c:c + 1]

    class Bufs:
        pass

    def alloc_norm(B, rstd, ntmp, sq=None):
        if sq is None:
            B.sq = [sb.alloc(f"sq{i}", T, BF16) for i in range(2)]
            B.sqk = [Tk(f"sq{i}") for i in range(2)]
        else:
            B.sq = [sq[0][0], sq[1][0]]
            B.sqk = [sq[0][1], sq[1][1]]
        B.rstd, B.rstdk = rstd
        B.ntmp = [ntmp[0][0], ntmp[1][0]]
        B.ntmpk = [ntmp[0][1], ntmp[1][1]]
        B.nti = 0

    def norm_rstd(B, src_fn, n, src_tks, src_psum=False):
        for kc in range(KC):
            sq, sqk = B.sq[kc % 2], B.sqk[kc % 2]
            tks = src_tks(kc)
            op("act", lambda: nc.scalar.activation(out=sq[:, 0:n], in_=src_fn(kc), func=AF.Square),
               rd=([] if src_psum else tks), wr=[sqk] + (tks if src_psum else []))
            op("pe", lambda: nc.tensor.matmul(PB[5][:, 0:n], lhsT=ones[:, :], rhs=sq[:, 0:n], start=(kc == 0), stop=(kc == KC - 1)),
               rd=[sqk, cbk], wr=[PBk[5]])
        op("act", lambda: nc.scalar.activation(out=B.rstd[:, 0:n], in_=PB[5][:, 0:n], func=AF.Ln, scale=1.0 / D, bias=EPS),
           wr=[B.rstdk, PBk[5]])
        op("act", lambda: nc.scalar.activation(out=B.rstd[:, 0:n], in_=B.rstd[:, 0:n], func=AF.Exp, scale=-0.5),
           rd=[B.rstdk], wr=[B.rstdk])

    def make_h(B, l, seq, which, c0, n, dst_fn, dst_tk):
        ti = c0 // T
        norm_rstd(B, lambda kc: Rap(seq, kc, c0, n), n, lambda kc: [seq.Rk[kc][ti]])
        for kc in range(KC):
            b = B.nti
            B.nti ^= 1
            tmp, tmpk = B.ntmp[b], B.ntmpk[b]
            op("dve", lambda: nc.vector.tensor_tensor(out=tmp[:, 0:n], in0=Rap(seq, kc, c0, n), in1=B.rstd[:, 0:n], op=ALU.mult),
               rd=[seq.Rk[kc][ti], B.rstdk], wr=[tmpk])
            if n >= 256:
                op("pool", lambda: nc.gpsimd.tensor_scalar(out=dst_fn(kc), in0=tmp[:, 0:n], scalar1=Gc(seq, 0 if which == 0 else 2, kc),
                                                           scalar2=Sc(l, seq, which, kc), op0=ALU.mult, op1=ALU.add),
                   rd=[tmpk, lvk, modk], wr=[dst_tk])
            else:
                op("act", lambda: nc.scalar.activation(out=dst_fn(kc), in_=tmp[:, 0:n], func=AF.Identity,
                                                       scale=Gc(seq, 0 if which == 0 else 2, kc), bias=Sc(l, seq, which, kc)),
                   rd=[tmpk, lvk, modk], wr=[dst_tk])

    def stage_h(B, l, seq, which, ti, H, mode):
        c0, n = seq.tiles[ti]
        W = H + T + H
        hb, hbk = B.hb, B.hbk
        make_h(B, l, seq, which, c0, n, lambda kc: hb[:, kc * W + H:kc * W + H + n], hbk)
        if c0 == 0:
            op("pool", lambda: nc.gpsimd.memset(V(hb, 0, [[W, KC], [1, H]]), 0.0), wr=[hbk])
        elif mode == "F":
            op("pool", lambda: nc.gpsimd.tensor_copy(out=V(hb, 0, [[W, KC], [1, H]]), in_=V(B.htail, 0, [[H, KC], [1, H]])),
               rd=[B.htailk], wr=[hbk])
        else:
            make_h(B, l, seq, which, c0 - H, H, lambda kc: hb[:, kc * W:kc * W + H], hbk)
        if c0 + n == seq.len:
            op("pool", lambda: nc.gpsimd.memset(V(hb, H + n, [[W, KC], [1, H]]), 0.0), wr=[hbk])
        else:
            make_h(B, l, seq, which, c0 + n, H, lambda kc: hb[:, kc * W + H + n:kc * W + H + n + H], hbk)
        if mode == "F" and c0 + n < seq.len:
            op("pool", lambda: nc.gpsimd.tensor_copy(out=V(B.htail, 0, [[H, KC], [1, H]]), in_=V(hb, n, [[W, KC], [1, H]])),
               rd=[hbk], wr=[B.htailk])

    class WStream:
        def __init__(self, name, nbuf, cols):
            self.b = [(sb.alloc(f"{name}{i}", cols, BF16), Tk(f"{name}{i}")) for i in range(nbuf)]
            self.i = 0

        def load(self, src, src_tk):
            t, tk = self.b[self.i]
            self.i = (self.i + 1) % len(self.b)
            S.dma("sp", out=t[:, :], in_=src, rd=[src_tk], wr=[tk])
            return t, tk

        def load2(self, src, src_tk, cols):
            t, tk = self.b[self.i]
            self.i = (self.i + 1) % len(self.b)
            S.dma("sp", out=t[:, 0:cols], in_=src, rd=[src_tk], wr=[tk])
            return t, tk

    def mm_chunk(wt, wtk, hb, hbk, W, H, n, bank, halo_bank=None, halo_col=0):
        for kc in range(KC):
            op("pe", lambda: nc.tensor.matmul(PB[bank][:, 0:n], lhsT=wt[:, kc * 128:(kc + 1) * 128],
                                              rhs=hb[:, kc * W + H:kc * W + H + n], start=(kc == 0), stop=(kc == KC - 1)),
               rd=[wtk, hbk], wr=[PBk[bank]], inc=(kc == KC - 1))
        if halo_bank is not None:
            for kc in range(KC):
                op("pe", lambda: nc.tensor.matmul(V(PB[halo_bank], halo_col, [[H, 2], [1, H]]), lhsT=wt[:, kc * 128:(kc + 1) * 128],
                                                  rhs=V(hb, kc * W, [[H + n, 2], [1, H]]), start=(kc == 0), stop=(kc == KC - 1)),
                   rd=[wtk, hbk], wr=[PBk[halo_bank]], inc=(kc == KC - 1))

    def evac_halo(dst, dstk, H, n, bank, halo_bank, halo_col):
        op("act", lambda: nc.scalar.activation(out=dst[:, H:H + n], in_=PB[bank][:, 0:n], func=AF.Identity), wr=[dstk, PBk[bank]])
        op("dve", lambda: nc.vector.tensor_copy(out=V(dst, 0, [[H + n, 2], [1, H]]), in_=V(PB[halo_bank], halo_col, [[H, 2], [1, H]])),
           wr=[dstk, PBk[halo_bank]])

    def post_norm_residual(B, l, seq, ti, which, cbase, nh, lhs_fn, lhs_tk_fn, rhs_fn, rhs_tk, nk, parts):
        nparts = len(parts)
        for oc in range(KC):
            bank = oc // 2
            col = (oc % 2) * 256
            for pi, (k0, k1) in enumerate(parts):
                wt, wtk = lhs_tk_fn(oc, pi)
                for k in range(k0, k1):
                    op("pe", lambda: nc.tensor.matmul(PB[bank][:, col:col + nh], lhsT=lhs_fn(wt, oc, k - k0), rhs=rhs_fn(k),
                                                      start=(k == 0), stop=(k == nk - 1)),
                       rd=[wtk, rhs_tk], wr=[PBk[bank]], inc=(k == k1 - 1))
        norm_rstd(B, lambda oc: PB[oc // 2][:, (oc % 2) * 256:(oc % 2) * 256 + nh], nh, lambda oc: [PBk[oc // 2]], src_psum=True)
        c0, n = seq.tiles[ti]
        for oc in range(KC):
            b = B.nti
            B.nti ^= 1
            tmp, tmpk = B.ntmp[b], B.ntmpk[b]
            col = (oc % 2) * 256
            op("dve", lambda: nc.vector.tensor_tensor(out=tmp[:, 0:nh], in0=PB[oc // 2][:, col:col + nh], in1=B.rstd[:, 0:nh], op=ALU.mult),
               rd=[B.rstdk], wr=[tmpk, PBk[oc // 2]])
            ra = Rap(seq, oc, c0 + cbase, nh)
            op("dve", lambda: nc.vector.scalar_tensor_tensor(out=ra, in0=tmp[:, 0:nh], scalar=Gc(seq, 1 if which == 0 else 3, oc),
                                                             in1=ra, op0=ALU.mult, op1=ALU.add),
               rd=[tmpk, lvk, seq.Rk[oc][ti]], wr=[seq.Rk[oc][ti]])

    def mix_alloc(l):
        B = Bufs()
        W = HM + T + HM
        B.W = W
        B.g = [sb.alloc(f"gt{i}", T, F32) for i in range(3)]
        B.gk = [Tk(f"gt{i}") for i in range(3)]
        B.g2 = [sb.alloc(f"gu{i}", T, F32) for i in range(3)]
        B.g2k = [Tk(f"gu{i}") for i in range(3)]
        alloc_norm(B, (B.g[2], B.gk[2]), [(B.g[0], B.gk[0]), (B.g[1], B.gk[1])])
        B.sel, B.selk = B.g[2], B.gk[2]
        B.hb = sb.alloc("hbm", KC * W, BF16)
        B.hbk = Tk("hbm")
        B.htail = sb.alloc("htail", KC * HM, BF16)
        B.htailk = Tk("htail")
        B.ws = WStream("wsm", 3, 1024)
        B.zx = [sb.alloc(f"zx{i}", W, F32) for i in range(2)]
        B.zxk = [Tk(f"zx{i}") for i in range(2)]
        B.xa = [sb.alloc(f"xa{i}", T, F32) for i in range(2)]
        B.xak = [Tk(f"xa{i}") for i in range(2)]
        B.xab = [sb.alloc(f"xab{i}", T, BF16) for i in range(2)]
        B.xabk = [Tk(f"xab{i}") for i in range(2)]
        B.hd = [sb.alloc(f"hd{i}", T, F32) for i in range(2)]
        B.hdk = [Tk(f"hd{i}") for i in range(2)]
        B.mixin = sb.alloc("mixin", 8 * T, BF16)
        B.mixk = Tk("mixin")
        B.ps1 = sb.alloc("ps1", W, F32)
        B.ps1k = Tk("ps1")
        B.ps2, B.ps2k = B.zx[0], B.zxk[0]
        B.pbf = sb.alloc("pbf", T, BF16)
        B.pbfk = Tk("pbf")
        B.ug = sb.alias("ug", 2 * T, F32, B.xa[0])
        B.ugk = [B.xak[0], B.xak[1]]
        B.vg = sb.alias("vg", 2 * T, BF16, B.xab[0])
        B.vgk = B.xabk
        B.vn = sb.alias("vn", 4 * 2 * 256, BF16, B.hd[0])
        B.vnk = B.hdk
        B.st = sb.alloc("st", 4 * 6 + 4 * 2 + 4, F32)
        B.stk = Tk("st")
        lp = l % 2
        lw[lp] = sb.alloc("lw", 16 * 128 + 2 * 128 + 4 * 128, BF16)
        lwk[lp] = Tk("lw")
        bst[lp] = sb.alloc("bst", 256, F32)
        bstk[lp] = Tk("bst")
        S.dma("pool", out=lw[lp][:, 0:2048], in_=lru_bd[l], wr=[lwk[lp]])
        S.dma("pool", out=lw[lp][:, 2048:2304], in_=pool_bd[l], wr=[lwk[lp]])
        S.dma("pool", out=lw[lp][:, 2304:2816], in_=wsT[l], wr=[lwk[lp]])
        S.dma("sp", out=bst[lp][:, :], in_=bsT[l], wr=[bstk[lp]])
        return B

    def xa_chunk(B, l, seq, ti, c):
        c0, n = seq.tiles[ti]
        W, H = B.W, HM
        lp = l % 2
        wt, wtk = B.ws.load(wb_in[l, c], tk_wb[("in", l, c)])
        bank = c % 2
        mm_chunk(wt, wtk, B.hb, B.hbk, W, H, n, bank, 2, (c % 2) * 2 * H)
        zx, zxk = B.zx[c % 2], B.zxk[c % 2]
        evac_halo(zx, zxk, H, n, bank, 2, (c % 2) * 2 * H)
        xa, xak = B.xa[c % 2], B.xak[c % 2]
        op("pool", lambda: nc.gpsimd.tensor_scalar(out=xa[:, 0:n], in0=zx[:, H - 2:H - 2 + n], scalar1=vcol(l, "conv_a_w", 0 * 4 + c),
                                                   scalar2=vcol(l, "conv_a_b", c), op0=ALU.mult, op1=ALU.add),
           rd=[zxk, vlk[lp]], wr=[xak])
        for k in range(1, 4):
            op("dve", lambda: nc.vector.scalar_tensor_tensor(out=xa[:, 0:n], in0=zx[:, H - 2 + k:H - 2 + k + n], scalar=vcol(l, "conv_a_w", k * 4 + c),
                                                             in1=xa[:, 0:n], op0=ALU.mult, op1=ALU.add),
               rd=[zxk, vlk[lp], xak], wr=[xak])
        op("dve", lambda: nc.vector.tensor_copy(out=B.xab[c % 2][:, 0:n], in_=xa[:, 0:n]), rd=[xak], wr=[B.xabk[c % 2]])

    def stage_scan_multi(B, l, seq, ti, items):
        c0, n = seq.tiles[ti]
        lp = l % 2
        ctxs = []
        for slot, (c, d, gset, init_ap, init_tks, out_state) in enumerate(items):
            g, gk = (B.g, B.gk) if gset == 0 else (B.g2, B.g2k)
            X = Bufs()
            X.c, X.d, X.j = c, d, d * 4 + c
            X.xab, X.xabk = B.xab[c % 2][:, 0:n], B.xabk[c % 2]
            X.xa, X.xak = B.xa[c % 2][:, 0:n], B.xak[c % 2]
            X.t = [t_[:, 0:n] for t_ in g]
            X.tf = g
            X.k = gk
            X.pa, X.px = (3, 4) if slot == 0 else (5, 7)
            X.init_ap, X.init_tks, X.out_state = init_ap, init_tks, out_state
            X.hd, X.hdk = B.hd[slot], B.hdk[slot]
            ctxs.append(X)
        col = lambda X, base: lv[:, base + X.j:base + X.j + 1]
        for X in ctxs:
            wa = lw[lp][:, ((X.d * 2 + 0) * 4 + X.c) * 128:((X.d * 2 + 0) * 4 + X.c) * 128 + 128]
            wx = lw[lp][:, ((X.d * 2 + 1) * 4 + X.c) * 128:((X.d * 2 + 1) * 4 + X.c) * 128 + 128]
            op("pe", lambda: nc.tensor.matmul(PB[X.pa][:, 0:n], lhsT=wa, rhs=X.xab, start=True, stop=True), rd=[lwk[lp], X.xabk], wr=[PBk[X.pa]])
            op("pe", lambda: nc.tensor.matmul(PB[X.px][:, 0:n], lhsT=wx, rhs=X.xab, start=True, stop=True), rd=[lwk[lp], X.xabk], wr=[PBk[X.px]])
        for X in ctxs:
            op("act", lambda: nc.scalar.activation(out=X.t[0], in_=PB[X.pa][:, 0:n], func=AF.Tanh, scale=0.5, bias=col(X, LV_HBA)), rd=[lvk], wr=[X.k[0], PBk[X.pa]])
        for X in ctxs:
            op("act", lambda: nc.scalar.activation(out=X.t[1], in_=PB[X.px][:, 0:n], func=AF.Tanh, scale=0.5, bias=col(X, LV_HBX)), rd=[lvk], wr=[X.k[1], PBk[X.px]])
        for X in ctxs:
            op("act", lambda: nc.scalar.activation(out=X.t[2], in_=X.t[0], func=AF.Exp, scale=col(X, LV_NSPH), bias=col(X, LV_NSPH)), rd=[X.k[0], lvk], wr=[X.k[2]])
        for X in ctxs:
            op("act", lambda: nc.scalar.activation(out=X.t[0], in_=X.t[0], func=AF.Tanh, scale=col(X, LV_NSPQ), bias=col(X, LV_NSPQ)), rd=[X.k[0], lvk], wr=[X.k[0]])
        for X in ctxs:
            op("dve", lambda: nc.vector.scalar_tensor_tensor(out=X.t[1], in0=X.t[1], scalar=1.0, in1=X.xa, op0=ALU.add, op1=ALU.mult), rd=[X.k[1], X.xak], wr=[X.k[1]])
        for X in ctxs:
            op("act", lambda: nc.scalar.activation(out=X.t[0], in_=X.t[0], func=AF.Sqrt, scale=-0.25), rd=[X.k[0]], wr=[X.k[0]])
        for X in ctxs:
            op("dve", lambda: nc.vector.scalar_tensor_tensor(out=X.t[0], in0=X.t[2], scalar=1.0, in1=X.t[0], op0=ALU.add, op1=ALU.mult), rd=[X.k[2], X.k[0]], wr=[X.k[0]])
        for X in ctxs:
            op("dve", lambda: nc.vector.tensor_tensor(out=X.t[1], in0=X.t[1], in1=X.t[0], op=ALU.mult), rd=[X.k[1], X.k[0]], wr=[X.k[1]])
        for X in ctxs:
            hd, hdk = X.hd, X.hdk
            if X.d == 0:
                op("dve", lambda: nc.vector.tensor_tensor_scan(out=hd[:, 0:n], data0=X.t[2], data1=X.t[1], initial=X.init_ap,
                                                               op0=ALU.mult, op1=ALU.add), rd=[X.k[2], X.k[1]] + X.init_tks, wr=[hdk])
                last = hd[:, n - 1:n]
            else:
                op("dve", lambda: nc.vector.tensor_tensor_scan(out=V(hd, n - 1, [[-1, n]]), data0=V(X.tf[2], n - 1, [[-1, n]]), data1=V(X.tf[1], n - 1, [[-1, n]]),
                                                               initial=X.init_ap, op0=ALU.mult, op1=ALU.add), rd=[X.k[2], X.k[1]] + X.init_tks, wr=[hdk])
                last = hd[:, 0:1]
            if X.out_state is not None:
                oap, otk = X.out_state
                op("pool", lambda: nc.gpsimd.tensor_copy(out=oap, in_=last), rd=[hdk], wr=[otk])

    def scan_item(seq, ti, c, d, gset, save_sinb):
        last_tile = (ti == len(seq.tiles) - 1)
        if seq is ctx:
            init_ap, init_tks = 0.0, []
            out_state = (stf[:, c:c + 1], stfk) if d == 0 else (stbc[:, c:c + 1], stbck)
        elif d == 0:
            init_ap, init_tks = stf[:, c:c + 1], [stfk]
            out_state = (stf[:, c:c + 1], stfk)
        else:
            if last_tile:
                init_ap, init_tks = stbc[:, c:c + 1], [stbck]
            else:
                init_ap, init_tks = sinb[:, ti * 4 + c:ti * 4 + c + 1], [sinbk[ti]]
            out_state = (sinb[:, (ti - 1) * 4 + c:(ti - 1) * 4 + c + 1], sinbk[ti - 1]) if (ti > 0 and save_sinb) else None
        return (c, d, gset, init_ap, init_tks, out_state)

    def gate_chunk(B, l, seq, ti, c):
        c0, n = seq.tiles[ti]
        wt, wtk = B.ws.load(wb_in[l, 4 + c], tk_wb[("in", l, 4 + c)])
        bank = 6 + (c % 2)
        mm_chunk(wt, wtk, B.hb, B.hbk, B.W, HM, n, bank)
        op("dve", lambda: nc.vector.tensor_tensor(out=B.hd[0][:, 0:n], in0=B.hd[0][:, 0:n], in1=B.hd[1][:, 0:n], op=ALU.add),
           rd=[B.hdk[0], B.hdk[1]], wr=[B.hdk[0]])
        g, gk = B.zx[c % 2], B.zxk[c % 2]
        op("act", lambda: nc.scalar.activation(out=g[:, 0:n], in_=PB[bank][:, 0:n], func=AF.Gelu_apprx_tanh), wr=[gk, PBk[bank]])
        op("dve", lambda: nc.vector.tensor_tensor(out=B.mixin[:, c * T:c * T + n], in0=g[:, 0:n], in1=B.hd[0][:, 0:n], op=ALU.mult),
           rd=[gk, B.hdk[0]], wr=[B.mixk])

    def stage_rest(B, l, seq, ti, nxt=None):
        c0, n = seq.tiles[ti]
        W, H = B.W, HM
        lp = l % 2
        nsub = n // 128
        p6 = PB[6][:].bitcast(BF16)
        st = B.st
        wso = 2304
        first = (c0 == 0)
        lastt = (c0 + n == seq.len)
        Wn = H + n + H

        def G1():
            for i in range(4):
                wt, wtk = B.ws.load(wb_in[l, 10 + i], tk_wb[("in", l, 10 + i)])
                bank = i % 2
                mm_chunk(wt, wtk, B.hb, B.hbk, W, H, n, bank)
                if i < 2:
                    op("act", lambda: nc.scalar.activation(out=B.ug[:, i * T:i * T + n], in_=PB[bank][:, 0:n], func=AF.Gelu_apprx_tanh),
                       wr=[B.ugk[i], PBk[bank]])
                else:
                    op("act", lambda: nc.scalar.activation(out=B.vg[:, (i - 2) * T:(i - 2) * T + n], in_=PB[bank][:, 0:n], func=AF.Gelu_apprx_tanh),
                       wr=B.vgk + [PBk[bank]])

        def G2():
            for sub in range(nsub):
                for vc in range(2):
                    op("pe", lambda: nc.tensor.transpose(out=p6[:, sub * 256 + vc * 128:sub * 256 + vc * 128 + 128],
                                                         in_=B.vg[:, vc * T + sub * 128:vc * T + sub * 128 + 128], identity=ident[:, :]),
                       rd=B.vgk + [cbk], wr=[PBk[6]])
            op("pool", lambda: nc.gpsimd.memset(B.vn[:, :], 0.0), wr=B.vnk)

        def G3():
            for sub in range(nsub):
                op("dve", lambda: nc.vector.bn_stats(out=st[:, sub * 6:sub * 6 + 6], in_=p6[:, sub * 256:sub * 256 + 256]), wr=[B.stk, PBk[6]])
            for sub in range(nsub):
                op("dve", lambda: nc.vector.bn_aggr(out=st[:, 24 + sub * 2:24 + sub * 2 + 2], in_=st[:, sub * 6:sub * 6 + 6]), rd=[B.stk], wr=[B.stk])
            op("act", lambda: nc.scalar.activation(out=st[:, 32:32 + nsub], in_=V(st, 25, [[2, nsub]]), func=AF.Ln, bias=EPS), rd=[B.stk], wr=[B.stk])
            op("act", lambda: nc.scalar.activation(out=st[:, 32:32 + nsub], in_=st[:, 32:32 + nsub], func=AF.Exp, scale=-0.5), rd=[B.stk], wr=[B.stk])

        def G4():
            for sub in range(nsub):
                for gc in range(2):
                    op("dve", lambda: nc.vector.tensor_scalar(out=V(B.vn, (sub * 2 + gc) * 256, [[192, 2], [1, 64]]),
                                                              in0=p6[:, sub * 256 + gc * 128:sub * 256 + gc * 128 + 128].rearrange("p (a b) -> p a b", b=64),
                                                              scalar1=st[:, 24 + sub * 2:24 + sub * 2 + 1], scalar2=st[:, 32 + sub:32 + sub + 1],
                                                              op0=ALU.subtract, op1=ALU.mult),
                       rd=[B.stk], wr=B.vnk + [PBk[6]])

        def G5(gc):
            bank = 7 if gc == 0 else 3
            for sub in range(nsub):
                for gh in range(2):
                    g = 2 * gc + gh
                    op("pe", lambda: nc.tensor.matmul(PB[bank][:, sub * 128:sub * 128 + 128],
                                                      lhsT=B.vn[:, (sub * 2 + gc) * 256 + gh * 128:(sub * 2 + gc) * 256 + gh * 128 + 128],
                                                      rhs=lw[lp][:, wso + g * 128:wso + g * 128 + 128], start=(gh == 0), stop=(gh == 1)),
                       rd=B.vnk + [lwk[lp]], wr=[PBk[bank]], inc=(gh == 1))
            g_, gk_ = B.g[gc], B.gk[gc]
            op("dve", lambda: nc.vector.scalar_tensor_tensor(out=g_[:, 0:n].rearrange("p (s i) -> p s i", i=128),
                                                             in0=PB[bank][:, 0:n].rearrange("p (s i) -> p s i", i=128),
                                                             scalar=vcol(l, "gmlp_norm", gc),
                                                             in1=V(bst[lp], gc * 128, [[0, nsub], [1, 128]]), op0=ALU.mult, op1=ALU.add),
               rd=[vlk[lp], bstk[lp]], wr=[gk_, PBk[bank]])
            op("dve", lambda: nc.vector.tensor_tensor(out=B.mixin[:, (6 + gc) * T:(6 + gc) * T + n], in0=g_[:, 0:n], in1=B.ug[:, gc * T:gc * T + n], op=ALU.mult),
               rd=[gk_, B.ugk[gc]], wr=[B.mixk])

        def add(eng, o, a_, b_, rd, wr):
            if eng == "pool":
                op("pool", lambda: nc.gpsimd.tensor_tensor(out=o, in0=a_, in1=b_, op=ALU.add), rd=rd, wr=wr)
            else:
                op("dve", lambda: nc.vector.tensor_tensor(out=o, in0=a_, in1=b_, op=ALU.add), rd=rd, wr=wr)

        def P1(pc):
            wt, wtk = B.ws.load(wb_in[l, 8 + pc], tk_wb[("in", l, 8 + pc)])
            bank = pc % 2
            mm_chunk(wt, wtk, B.hb, B.hbk, W, H, n, bank, 2, pc * 2 * H)
            evac_halo(B.zx[pc], B.zxk[pc], H, n, bank, 2, pc * 2 * H)

        def P2(pc):
            zp, zpk = B.zx[pc], B.zxk[pc]
            if pc == 0:
                add("dve", V(B.sel, 0, [[1, n]], 0, 64), V(zp, H - 1, [[1, n]], 0, 64), V(zp, H, [[1, n]], 0, 64), [zpk], [B.selk])
                add("dve", V(B.ps1, 1, [[1, Wn - 1]], 64, 64), V(zp, 0, [[1, Wn - 1]], 64, 64), V(zp, 1, [[1, Wn - 1]], 64, 64), [zpk], [B.ps1k])
                add("dve", V(B.sel, 0, [[1, n]], 64, 64), V(B.ps1, H - 1, [[1, n]], 64, 64), V(B.ps1, H + 1, [[1, n]], 64, 64), [B.ps1k], [B.selk])
            else:
                add("pool", V(B.ps1, 1, [[1, Wn - 1]]), V(zp, 0, [[1, Wn - 1]]), V(zp, 1, [[1, Wn - 1]]), [zpk], [B.ps1k])
                add("pool", V(B.ps2, 2, [[1, Wn - 3]]), V(B.ps1, 1, [[1, Wn - 3]]), V(B.ps1, 3, [[1, Wn - 3]]), [B.ps1k], [B.ps2k])
                add("dve", V(B.sel, 0, [[1, n]], 0, 64), V(B.ps2, H - 2, [[1, n]], 0, 64), V(B.ps2, H + 2, [[1, n]], 0, 64), [B.ps2k], [B.selk])
                add("dve", V(B.ps1, 4, [[1, Wn - 7]], 64, 64), V(B.ps2, 2, [[1, Wn - 7]], 64, 64), V(B.ps2, 6, [[1, Wn - 7]], 64, 64), [B.ps2k], [B.ps1k])
                add("dve", V(B.sel, 0, [[1, n]], 64, 64), V(B.ps1, H - 4, [[1, n]], 64, 64), V(B.ps1, H + 4, [[1, n]], 64, 64), [B.ps1k], [B.selk])
            op("dve", lambda: nc.vector.scalar_tensor_tensor(out=B.pbf[:, 0:n], in0=B.sel[:, 0:n], scalar=cst[:, CO_INVW + pc:CO_INVW + pc + 1],
                                                             in1=zp[:, H:H + n], op0=ALU.mult, op1=ALU.subtract),
               rd=[B.selk, zpk, cstk], wr=[B.pbfk])
            for edge, cc, co in ((first, 0, CO_EL), (lastt, n - 8, CO_ER)):
                if edge:
                    op("dve", lambda: nc.vector.tensor_tensor(out=B.sel[:, cc:cc + 8], in0=B.sel[:, cc:cc + 8], in1=cst[:, co + pc * 8:co + pc * 8 + 8], op=ALU.mult),
                       rd=[B.selk, cstk], wr=[B.selk])
                    op("dve", lambda: nc.vector.tensor_tensor(out=B.pbf[:, cc:cc + 8], in0=B.sel[:, cc:cc + 8], in1=zp[:, H + cc:H + cc + 8], op=ALU.subtract),
                       rd=[B.selk, zpk], wr=[B.pbfk])

        def P3(pc):
            op("pe", lambda: nc.tensor.matmul(PB[4][:, 0:n], lhsT=lw[lp][:, 2048 + pc * 128:2048 + pc * 128 + 128], rhs=B.pbf[:, 0:n], start=True, stop=True),
               rd=[lwk[lp], B.pbfk], wr=[PBk[4]])
            op("dve", lambda: nc.vector.tensor_scalar(out=B.mixin[:, (4 + pc) * T:(4 + pc) * T + n], in0=PB[4][:, 0:n],
                                                      scalar1=vcol(l, "pool_b", pc), scalar2=vcol(l, "pool_scale", pc), op0=ALU.add, op1=ALU.mult),
               rd=[vlk[lp]], wr=[B.mixk, PBk[4]])

        G1(); P1(0); G2(); P1(1); G3(); P2(0); G4(); P3(0); P2(1); G5(0); P3(1); G5(1)
        dbg(f"mixin_{seq.name}{ti}_l{l}", B.mixin[:, 0:8 * T], [B.mixk], 8 * T)
        if nxt is not None:
            stage_h(B, l, nxt[0], 0, nxt[1], HM, nxt[2])
        nh = n // 2
        for hh in range(2):
            cb_ = hh * nh
            post_norm_residual(B, l, seq, ti, 0, cb_, nh,
                               lambda wt, oc, k: wt[:, k * 128:(k + 1) * 128],
                               lambda oc, pi: B.ws.load(wb_out[l, oc], tk_wb[("out", l, oc)]),
                               lambda k: B.mixin[:, k * T + cb_:k * T + cb_ + nh], B.mixk, KC, [(0, KC)])

    def mix_tile(B, l, seq, ti, mode, dirs, full, nxt=None, need_h=True):
        if need_h:
            stage_h(B, l, seq, 0, ti, HM, mode)
        if full:
            dbg(f"h_{seq.name}{ti}_l{l}", B.hb[:, 0:KC * B.W], [B.hbk], KC * B.W)
        xa_chunk(B, l, seq, ti, 0)
        if len(dirs) == 2:
            for c in range(4):
                if c < 3:
                    xa_chunk(B, l, seq, ti, c + 1)
                stage_scan_multi(B, l, seq, ti, [scan_item(seq, ti, c, 0, 0, not full), scan_item(seq, ti, c, 1, 1, not full)])
                if full:
                    gate_chunk(B, l, seq, ti, c)
        else:
            d = dirs[0]
            xa_chunk(B, l, seq, ti, 1)
            for c in (0, 2):
                stage_scan_multi(B, l, seq, ti, [scan_item(seq, ti, c, d, 0, not full), scan_item(seq, ti, c + 1, d, 1, not full)])
                if c == 0:
                    xa_chunk(B, l, seq, ti, 2)
                    xa_chunk(B, l, seq, ti, 3)
                    if nxt is not None:
                        stage_h(B, l, nxt[0], 0, nxt[1], HM, nxt[2])
        if full:
            stage_rest(B, l, seq, ti, nxt)
        elif len(dirs) == 2 and nxt is not None:
            stage_h(B, l, nxt[0], 0, nxt[1], HM, nxt[2])

    def mix_phase(l, last):
        B = mix_alloc(l)
        first_lat = (lat, ntl - 1, "B") if ntl > 1 else (lat, 0, "F")
        mix_tile(B, l, ctx, 0, "B", (0, 1), not last, nxt=first_lat)
        for ti in range(ntl - 1, 0, -1):
            mix_tile(B, l, lat, ti, "B", (1,), False, nxt=((lat, ti - 1, "B") if ti > 1 else (lat, 0, "F")), need_h=False)
        for ti in range(ntl):
            mix_tile(B, l, lat, ti, "F", (0, 1), True, nxt=((lat, ti + 1, "F") if ti + 1 < ntl else None), need_h=False)
        return B

    def ffn_alloc():
        B = Bufs()
        W = HF + T + HF
        B.W = W
        B.gt = [[sb.alloc(f"gtf{p}{br}", T, F32) for br in range(2)] for p in range(2)]
        B.gtk = [[Tk(f"gtf{p}{br}") for br in range(2)] for p in range(2)]
        B.dg = WStream("dgs", 2, 2 * NPE * 128)
        rstd = sb.alias("rstdf", T, F32, B.dg.b[0][0])
        sqa = [(sb.alias(f"sqf{i}", T, BF16, B.gt[1][i]), B.gtk[1][i]) for i in range(2)]
        alloc_norm(B, (rstd, B.dg.b[0][1]), [(B.gt[0][0], B.gtk[0][0]), (B.gt[0][1], B.gtk[0][1])], sqa)
        B.hb = sb.alloc("hbf", KC * W, BF16)
        B.hbk = Tk("hbf")
        B.htail = sb.alloc("htailf", KC * HF, BF16)
        B.htailk = Tk("htailf")
        B.ws = WStream("wsu", 3, 1024)
        B.wd = WStream("wsd", 2, 1024)
        B.wd.b = B.wd.b + B.ws.b
        B.ZW = 1 + 10 * 65
        B.zb = [[sb.alloc(f"zb{p}{br}", B.ZW, BF16) for br in range(2)] for p in range(2)]
        B.zbk = [[Tk(f"zb{p}{br}") for br in range(2)] for p in range(2)]
        zero_zb(B)
        B.a = sb.alloc("aff", NDN * T, BF16)
        B.ak = Tk("aff")
        return B

    CONV_BANKS = [[3, 5], [6, 7]]
    DNP = [(0, 8), (8, 16), (16, 22)]

    def zero_zb(B):
        for p in range(2):
            for br in range(2):
                op("pool", lambda: nc.gpsimd.memset(B.zb[p][br][:, :], 0.0), wr=[B.zbk[p][br]])

    def ffn_tile(B, l, seq, ti, nxt=None, need_h=True):
        c0, n = seq.tiles[ti]
        W, H = B.W, HF
        lp = l % 2
        if need_h:
            stage_h(B, l, seq, 1, ti, H, "F")
        nrows = n // 64
        taps = TAPS if seq.grid else [(0, 0), (0, -1), (0, 1)]

        def up(j):
            p = j % 2
            for br in range(2):
                oc = j + br * NDN
                q = 2 * j + br
                wt, wtk = B.ws.load(wb_up[l, oc], tk_wb[("up", l, oc)])
                bank = q % 3
                hcol = (p * 2 + br) * 2 * H
                mm_chunk(wt, wtk, B.hb, B.hbk, W, H, n, bank, 4, hcol)
                zb, zbk = B.zb[p][br], B.zbk[p][br]
                if seq.grid:
                    op("act", lambda: nc.scalar.activation(out=V(zb, 1 + 65, [[65, nrows], [1, 64]]),
                                                           in_=PB[bank][:, 0:n].rearrange("p (r c) -> p r c", c=64), func=AF.Identity),
                       wr=[zbk, PBk[bank]])
                    op("dve", lambda: nc.vector.tensor_copy(out=V(zb, 1, [[(nrows + 1) * 65, 2], [1, 64]]), in_=V(PB[4], hcol, [[64, 2], [1, 64]])),
                       wr=[zbk, PBk[4]])
                else:
                    op("act", lambda: nc.scalar.activation(out=zb[:, H:H + n], in_=PB[bank][:, 0:n], func=AF.Identity), wr=[zbk, PBk[bank]])
                    op("dve", lambda: nc.vector.tensor_copy(out=V(zb, 0, [[H + n, 2], [1, H]]), in_=V(PB[4], hcol, [[H, 2], [1, H]])),
                       wr=[zbk, PBk[4]])

        def conv(j):
            p = j % 2
            dg, dgk = B.dg.load(dgs[l, j], tk_dg[(l, j)])
            taps_pe = [tp_ for tp_ in taps if tp_ in TAPS[:NPE]]
            taps_dve = [tp_ for tp_ in taps if tp_ not in TAPS[:NPE]]

            def src(zb, dr, dc):
                if seq.grid:
                    return V(zb, 1 + (1 + dr) * 65 + dc, [[65, nrows], [1, 64]])
                return zb[:, H + dc:H + dc + n]

            srcs = []
            for br in range(2):
                zb, zbk = B.zb[p][br], B.zbk[p][br]
                cb = CONV_BANKS[p][br]
                for t, (dr, dc) in enumerate(taps_pe):
                    q = br * NPE + TAPS.index((dr, dc))
                    op("pe", lambda: nc.tensor.matmul(PB[cb][:, 0:n], lhsT=dg[:, q * 128:(q + 1) * 128], rhs=src(zb, dr, dc),
                                                      start=(t == 0), stop=(t == len(taps_pe) - 1)),
                       rd=[dgk, zbk], wr=[PBk[cb]], inc=(t == len(taps_pe) - 1))
                gt, gtk = B.gt[p][br], B.gtk[p][br]
                srcs.append((gt[:, 0:n], [gtk], []) if taps_dve else (PB[cb][:, 0:n], [], [PBk[cb]]))
            for t, (dr, dc) in enumerate(taps_dve):
                for br in range(2):
                    zb, zbk = B.zb[p][br], B.zbk[p][br]
                    cb = CONV_BANKS[p][br]
                    oc = j + br * NDN
                    gt, gtk = B.gt[p][br], B.gtk[p][br]
                    g3 = gt[:, 0:n].rearrange("p (r c) -> p r c", c=64) if seq.grid else gt[:, 0:n]
                    wcol = vcol(l, "ffn_conv_w", ((dr + 1) * 3 + (dc + 1)) * NUP + oc)
                    if t == 0:
                        prev = PB[cb][:, 0:n].rearrange("p (r c) -> p r c", c=64) if seq.grid else PB[cb][:, 0:n]
                        op("dve", lambda: nc.vector.scalar_tensor_tensor(out=g3, in0=src(zb, dr, dc), scalar=wcol, in1=prev, op0=ALU.mult, op1=ALU.add),
                           rd=[zbk, vlk[lp]], wr=[gtk, PBk[cb]])
                    else:
                        op("dve", lambda: nc.vector.scalar_tensor_tensor(out=g3, in0=src(zb, dr, dc), scalar=wcol, in1=g3, op0=ALU.mult, op1=ALU.add),
                           rd=[zbk, vlk[lp], gtk], wr=[gtk])
            ocA, ocB = j, j + NDN
            gtA, gtAk = B.gt[p][0], B.gtk[p][0]
            (sA, sArd, sAwr), (sB, sBrd, sBwr) = srcs
            op("act", lambda: nc.scalar.activation(out=gtA[:, 0:n], in_=sA, func=AF.Gelu_apprx_tanh, bias=vcol(l, "ffn_conv_b", ocA)),
               rd=[vlk[lp]] + sArd, wr=[gtAk] + sAwr)
            op("dve", lambda: nc.vector.scalar_tensor_tensor(out=B.a[:, j * T:j * T + n], in0=sB, scalar=vcol(l, "ffn_conv_b", ocB),
                                                             in1=gtA[:, 0:n], op0=ALU.add, op1=ALU.mult),
               rd=[gtAk, vlk[lp]] + sBrd, wr=[B.ak] + sBwr)

        for j in range(NDN):
            up(j)
            if j > 0:
                conv(j - 1)
        conv(NDN - 1)
        if nxt is not None:
            stage_h(B, l, nxt[0], 1, nxt[1], H, "F")
        nh = n // 2
        for hh in range(2):
            cb_ = hh * nh
            post_norm_residual(B, l, seq, ti, 1, cb_, nh,
                               lambda wt, oc, k: wt[:, k * 128:(k + 1) * 128],
                               lambda oc, pi: B.wd.load2(wb_dn[l, oc, :, DNP[pi][0] * 128:DNP[pi][1] * 128], tk_wb[("dn", l, oc)], (DNP[pi][1] - DNP[pi][0]) * 128),
                               lambda k: B.a[:, k * T + cb_:k * T + cb_ + nh], B.ak, NDN, DNP)

    def ffn_phase(l, last):
        B = ffn_alloc()
        if not last:
            ffn_tile(B, l, ctx, 0)
            zero_zb(B)
        for ti in range(ntl):
            ffn_tile(B, l, lat, ti, nxt=((lat, ti + 1) if ti + 1 < ntl else None), need_h=(ti == 0))

    for l in range(depth):
        last = (l == depth - 1)
        if l > 0:
            S.new_epoch()
        layer_setup(l)
        if l + 1 < depth:
            convert_layer(l + 1)
        m = sb.mark()
        mix_phase(l, last)
        S.barrier()
        sb.release(m)
        m = sb.mark()
        ffn_phase(l, last)
        S.barrier()
        sb.release(m)

    osem = es.enter_context(nc.semaphore("osem"))
    nout = 0
    for kc in range(KC):
        for i in range(ntl):
            e = S.engs["sp"]
            S._deps(e, "sp", [Rk[kc][i]], [], True)
            nc.sync.dma_start(out=outT[kc * 128:(kc + 1) * 128, i * T:(i + 1) * T], in_=Rap(lat, kc, i * T, T)).then_inc(osem, 16)
            nout += 1
    nc.sync.wait_ge(osem, 16 * nout)
    S.barrier()
    info = dict(n_ins=S.n_ins, n_wait=S.n_wait, sb_peak=sb.peak, dbg=dbg_map)
    return nc, info


def _fm(v):
    return np.ascontiguousarray(np.asarray(v, np.float32).reshape(-1, 128).T)


def _chunk_w(w, noc):
    L = w.shape[0]
    w = np.asarray(w, np.float32).reshape(L, 8, 128, noc, 128)
    return np.ascontiguousarray(w.transpose(0, 3, 2, 1, 4).reshape(L, noc, 128, 1024))


def prepare_shared(depth, w_mod, b_mod, g_pre_mix, g_post_mix, g_pre_ffn, g_post_ffn, w_in, conv_a_w, conv_a_b,
                   lru_w_a, lru_b_a, lru_w_x, lru_b_x, lru_lam, pool_w, pool_b, pool_scale, gmlp_norm,
                   gmlp_w_s, gmlp_b_s, w_out, ffn_w_up, ffn_conv_w, ffn_conv_b, ffn_w_down):
    L = depth
    vecs = np.zeros((L, 128, NV), np.float32)
    for l in range(L):
        def put(name, arr):
            a = _fm(arr)
            vecs[l, :, VO[name]:VO[name] + a.shape[1]] = a
        put("b_mod", b_mod[l])
        put("g_pre_mix", g_pre_mix[l]); put("g_post_mix", g_post_mix[l])
        put("g_pre_ffn", g_pre_ffn[l]); put("g_post_ffn", g_post_ffn[l])
        vecs[l, :, VO["conv_a_w"]:VO["conv_a_w"] + 16] = np.concatenate([_fm(conv_a_w[l][k]) for k in range(4)], axis=1)
        put("conv_a_b", conv_a_b[l])
        for nm, arr in (("lru_b_a", lru_b_a), ("lru_b_x", lru_b_x), ("lru_lam", lru_lam)):
            vecs[l, :, VO[nm]:VO[nm] + 8] = np.concatenate([_fm(arr[l][d]) for d in range(2)], axis=1)
        put("pool_b", pool_b[l]); put("pool_scale", pool_scale[l]); put("gmlp_norm", gmlp_norm[l])
        vecs[l, :, VO["ffn_conv_w"]:VO["ffn_conv_w"] + 9 * NUP] = np.concatenate(
            [_fm(ffn_conv_w[l][kh, kw]) for kh in range(3) for kw in range(3)], axis=1)
        put("ffn_conv_b", ffn_conv_b[l])
    consts = np.zeros((128, NCONST), np.float32)
    consts[:, CO_ID:CO_ID + 128] = np.eye(128, dtype=np.float32)
    wins = [[2, 4], [8, 16]]
    for pc in range(2):
        for half in range(2):
            w = wins[pc][half]
            ps = slice(half * 64, half * 64 + 64)
            consts[ps, CO_INVW + pc] = 1.0 / w
            for t in range(8):
                consts[ps, CO_EL + pc * 8 + t] = 1.0 / min(w, t + w // 2)
                j = 7 - t
                consts[ps, CO_ER + pc * 8 + t] = 1.0 / min(w, j + 1 + w // 2)
    w_mod_c = np.ascontiguousarray(np.asarray(w_mod[:L], np.float32))
    w_in_r = _chunk_w(w_in[:L], 14)
    w_out_r = _chunk_w(w_out[:L], 8)
    w_up_r = _chunk_w(ffn_w_up[:L], NUP)
    wd = np.asarray(ffn_w_down[:L], np.float32).reshape(L, NDN, 128, 8, 128)
    w_dn_r = np.ascontiguousarray(wd.transpose(0, 3, 2, 1, 4).reshape(L, 8, 128, NDN * 128))
    lru_bd = np.zeros((L, 128, 2, 2, 4, 128), np.float32)
    for gi, wsrc in enumerate((lru_w_a, lru_w_x)):
        ws = np.asarray(wsrc[:L], np.float32)
        for c in range(4):
            for hh in range(2):
                lru_bd[:, hh * 64:hh * 64 + 64, :, gi, c, hh * 64:hh * 64 + 64] = ws[:, :, 2 * c + hh].transpose(0, 2, 1, 3)
    lru_bd = np.ascontiguousarray(lru_bd.reshape(L, 128, 16 * 128))
    pool_bdm = np.zeros((L, 128, 2, 128), np.float32)
    pw = np.asarray(pool_w[:L], np.float32)
    for pc in range(2):
        for hh in range(2):
            pool_bdm[:, hh * 64:hh * 64 + 64, pc, hh * 64:hh * 64 + 64] = pw[:, 2 * pc + hh]
    pool_bdm = np.ascontiguousarray(pool_bdm.reshape(L, 128, 256))
    wsTm = np.ascontiguousarray(np.asarray(gmlp_w_s[:L], np.float32).transpose(0, 3, 1, 2).reshape(L, 128, 512))
    bs = np.asarray(gmlp_b_s[:L], np.float32)
    bsTm = np.zeros((L, 128, 2, 128), np.float32)
    for gc in range(2):
        for hh in range(2):
            bsTm[:, hh * 64:hh * 64 + 64, gc, :] = bs[:, 2 * gc + hh][:, None, :]
    bsTm = np.ascontiguousarray(bsTm.reshape(L, 128, 256))
    return dict(vecs=vecs, consts=consts, w_mod=w_mod_c, w_in_r=w_in_r, w_out_r=w_out_r, w_up_r=w_up_r, w_dn_r=w_dn_r,
                lru_bd=lru_bd, pool_bd=pool_bdm, wsT=wsTm, bsT=bsTm)


def per_core_inputs(x, c, ctx, c_ctx, b):
    cv = np.stack([_fm(c[b]), _fm(c_ctx)], axis=2).reshape(128, 16)
    return dict(xT=np.ascontiguousarray(np.asarray(x[b], np.float32).T),
                cT=np.ascontiguousarray(np.asarray(ctx[b], np.float32).T),
                cvec=np.ascontiguousarray(cv.astype(np.float32)))


_CACHE = {}


def run(depth, x, c, ctx, c_ctx, weights, dbg_names=(), cores=None):
    Bn, SEQ, _ = x.shape
    CTXL = ctx.shape[1]
    key = (depth, SEQ, CTXL, tuple(dbg_names))
    if key not in _CACHE:
        _CACHE[key] = build(depth, SEQ, CTXL, dbg_names)
    nc, info = _CACHE[key]
    shared = prepare_shared(depth, **weights)
    cores = list(range(Bn)) if cores is None else cores
    in_maps = []
    for b in cores:
        m = dict(shared)
        m.update(per_core_inputs(x, c, ctx, c_ctx, b))
        in_maps.append(m)
    res = run_bass_kernel_spmd(nc, in_maps, core_ids=list(range(len(cores))))
    outs = [np.ascontiguousarray(r["outT"].T) for r in res.results]
    dbg = [r.get("dbg") for r in res.results] if dbg_names else None
    return np.stack(outs, axis=0), info, dbg


def kernel(x, c, ctx, c_ctx, **weights):
    x = np.asarray(x)
    out, _info, _ = run(4, x, np.asarray(c), np.asarray(ctx), np.asarray(c_ctx), {k: np.asarray(v) for k, v in weights.items()})
    return out.astype(np.float32)
```
